# Optimizing a Trainium2 kernel written in Bass

```python
import math
import jax, jax.numpy as jnp
from jax import lax
import numpy as np

D_MODEL = 1024
BATCH = 8
SEQ = 4096
DEPTH = 1

A_HEADS = 8
A_HEAD_DIM = 64
A_WIDTH = A_HEADS * A_HEAD_DIM
MOBA_BLOCK = 256
MOBA_TOPK = 3
MOBA_Q_CHUNK = 32
B_HEADS = 8
B_NOPE = 64
B_ROPE = 32
B_V = 64
B_WIDTH = B_HEADS * B_V
Q_LORA = 256
KV_LORA = 128
MLA_Q_BLOCK = 128
PEER_HEADS = 8
PEER_NKEYS = 128
PEER_EXPERTS = PEER_NKEYS * PEER_NKEYS
PEER_HALF = 128
PEER_QDIM = 2 * PEER_HALF
PEER_TOPK = 16
PEER_TOKEN_CHUNK = 128
ROPE_THETA = 10000.0
RMS_EPS = 1e-6
IN_WIDTHS = (A_WIDTH, A_WIDTH, A_WIDTH, Q_LORA, KV_LORA, B_ROPE, D_MODEL, D_MODEL)
IN_TOTAL = 3 * A_WIDTH + Q_LORA + KV_LORA + B_ROPE + 2 * D_MODEL

kernel_name = "hybrid_moba_mla_peer_block"


def rms_norm(x, g):
    xf = x.astype(jnp.float32)
    y = xf * lax.rsqrt(jnp.mean(xf * xf, axis=-1, keepdims=True) + RMS_EPS)
    return (y * g.astype(jnp.float32)).astype(x.dtype)


def rope_tables(positions, dim, dtype):
    inv_freq = ROPE_THETA ** (-jnp.arange(0, dim, 2, dtype=jnp.float32) / dim)
    ang = positions.astype(jnp.float32)[..., None] * inv_freq
    return jnp.cos(ang)[:, :, None, :].astype(dtype), jnp.sin(ang)[:, :, None, :].astype(dtype)


def apply_rope(x, cos, sin):
    x1, x2 = jnp.split(x, 2, axis=-1)
    return jnp.concatenate([x1 * cos - x2 * sin, x2 * cos + x1 * sin], axis=-1)


def moba_attention(q, k, v):
    B, S, H, dh = q.shape
    nb = -(-S // MOBA_BLOCK)
    pad = nb * MOBA_BLOCK - S
    k_eff = min(MOBA_TOPK, nb)
    scale = 1.0 / math.sqrt(dh)
    qt = q.transpose(0, 2, 1, 3)
    kt = jnp.pad(k.transpose(0, 2, 1, 3), ((0, 0), (0, 0), (0, pad), (0, 0)))
    vt = jnp.pad(v.transpose(0, 2, 1, 3), ((0, 0), (0, 0), (0, pad), (0, 0)))
    kb = kt.reshape(B, H, nb, MOBA_BLOCK, dh)
    vb = vt.reshape(B, H, nb, MOBA_BLOCK, dh)
    k_mean = jnp.mean(kb.astype(jnp.float32), axis=3)
    b_ix = jnp.arange(B)[:, None, None, None]
    h_ix = jnp.arange(H)[None, :, None, None]
    n_chunks = S // MOBA_Q_CHUNK

    def chunk(ci):
        start = ci * MOBA_Q_CHUNK
        blk = start // MOBA_BLOCK
        qc = lax.dynamic_slice_in_dim(qt, start, MOBA_Q_CHUNK, axis=2)
        gate = jnp.einsum('bhqd,bhnd->bhqn', qc.astype(jnp.float32), k_mean)
        gate = jnp.where(jnp.arange(nb) < blk, gate, -jnp.inf)
        _, sel = lax.top_k(gate, k_eff)
        sel_valid = jnp.arange(k_eff) < blk
        k_sel = kb[b_ix, h_ix, sel]
        v_sel = vb[b_ix, h_ix, sel]
        s_sel = jnp.einsum('bhqd,bhqtkd->bhqtk', qc, k_sel).astype(jnp.float32) * scale
        s_sel = jnp.where(sel_valid[:, None], s_sel, -jnp.inf)
        s_sel = s_sel.reshape(B, H, MOBA_Q_CHUNK, k_eff * MOBA_BLOCK)
        k_own = lax.dynamic_index_in_dim(kb, blk, axis=2, keepdims=False)
        v_own = lax.dynamic_index_in_dim(vb, blk, axis=2, keepdims=False)
        s_own = jnp.einsum('bhqd,bhkd->bhqk', qc, k_own).astype(jnp.float32) * scale
        q_pos = start + jnp.arange(MOBA_Q_CHUNK)
        k_pos = blk * MOBA_BLOCK + jnp.arange(MOBA_BLOCK)
        s_own = jnp.where(k_pos[None, :] <= q_pos[:, None], s_own, -jnp.inf)
        p = jax.nn.softmax(jnp.concatenate([s_sel, s_own], axis=-1), axis=-1).astype(v.dtype)
        p_sel = p[..., :k_eff * MOBA_BLOCK].reshape(B, H, MOBA_Q_CHUNK, k_eff, MOBA_BLOCK)
        p_own = p[..., k_eff * MOBA_BLOCK:]
        return (jnp.einsum('bhqtk,bhqtkd->bhqd', p_sel, v_sel)
                + jnp.einsum('bhqk,bhkd->bhqd', p_own, v_own))

    out = lax.map(chunk, jnp.arange(n_chunks))
    return out.transpose(1, 0, 3, 2, 4).reshape(B, S, H * dh)


def causal_attention_blocks(q, k, v, scale):
    B, H, S, _ = q.shape
    dv = v.shape[-1]
    k_pos = jnp.arange(S)

    def block(bi):
        qb = lax.dynamic_slice_in_dim(q, bi * MLA_Q_BLOCK, MLA_Q_BLOCK, axis=2)
        s = jnp.einsum('bhqd,bhkd->bhqk', qb, k).astype(jnp.float32) * scale
        q_pos = bi * MLA_Q_BLOCK + jnp.arange(MLA_Q_BLOCK)
        s = jnp.where(k_pos[None, :] <= q_pos[:, None], s, -jnp.inf)
        p = jax.nn.softmax(s, axis=-1).astype(v.dtype)
        return jnp.einsum('bhqk,bhkd->bhqd', p, v)

    out = lax.map(block, jnp.arange(S // MLA_Q_BLOCK))
    return out.transpose(1, 0, 3, 2, 4).reshape(B, S, H * dv)


def mla_attention(c_q, c_kv, k_pe, q_norm_g, w_q_up, kv_norm_g, w_kv_up, cos_b, sin_b):
    B, S, _ = c_q.shape
    q = (rms_norm(c_q, q_norm_g) @ w_q_up).reshape(B, S, B_HEADS, B_NOPE + B_ROPE)
    q_nope, q_pe = q[..., :B_NOPE], q[..., B_NOPE:]
    q_pe = apply_rope(q_pe, cos_b, sin_b)
    kv = (rms_norm(c_kv, kv_norm_g) @ w_kv_up).reshape(B, S, B_HEADS, B_NOPE + B_V)
    k_nope, v = kv[..., :B_NOPE], kv[..., B_NOPE:]
    k_pe = apply_rope(k_pe[:, :, None, :], cos_b, sin_b)
    k_pe = jnp.broadcast_to(k_pe, (B, S, B_HEADS, B_ROPE))
    q_full = jnp.concatenate([q_nope, q_pe], axis=-1).transpose(0, 2, 1, 3)
    k_full = jnp.concatenate([k_nope, k_pe], axis=-1).transpose(0, 2, 1, 3)
    v = v.transpose(0, 2, 1, 3)
    return causal_attention_blocks(q_full, k_full, v, 1.0 / math.sqrt(B_NOPE + B_ROPE))


def peer_ffn(h, w_query, sub_keys_1, sub_keys_2, expert_u, expert_v):
    B, S, D = h.shape
    T = B * S
    hc = h.reshape(T // PEER_TOKEN_CHUNK, PEER_TOKEN_CHUNK, D)

    def chunk(xc):
        q = (xc @ w_query).reshape(PEER_TOKEN_CHUNK, PEER_HEADS, 2, PEER_HALF)
        s1 = jnp.einsum('thd,hnd->thn', q[:, :, 0], sub_keys_1).astype(jnp.float32)
        s2 = jnp.einsum('thd,hnd->thn', q[:, :, 1], sub_keys_2).astype(jnp.float32)
        v1, i1 = lax.top_k(s1, PEER_TOPK)
        v2, i2 = lax.top_k(s2, PEER_TOPK)
        cand = (v1[..., :, None] + v2[..., None, :]).reshape(PEER_TOKEN_CHUNK, PEER_HEADS, PEER_TOPK * PEER_TOPK)
        cand_idx = (i1[..., :, None] * PEER_NKEYS + i2[..., None, :]).reshape(PEER_TOKEN_CHUNK, PEER_HEADS, PEER_TOPK * PEER_TOPK)
        best, pos = lax.top_k(cand, PEER_TOPK)
        idx = jnp.take_along_axis(cand_idx, pos, axis=-1)
        g = jax.nn.softmax(best, axis=-1)
        u = expert_u[idx]
        act = jax.nn.gelu(jnp.einsum('td,thkd->thk', xc, u).astype(jnp.float32), approximate=False)
        w = (g * act).astype(h.dtype)
        return jnp.einsum('thk,thkd->td', w, expert_v[idx])

    return lax.map(chunk, hc).reshape(B, S, D)


def setup_inputs(seed: int = 0) -> dict:
    key = jax.random.key(seed)
    ks = jax.random.split(key, 20)
    f32 = jnp.float32
    L = DEPTH

    def nrm(k, shape, scale):
        return jax.random.normal(k, shape, f32) * scale

    def gain(k, shape):
        return 1.0 + 0.02 * jax.random.normal(k, shape, f32)

    x = jax.random.normal(ks[0], (BATCH, SEQ, D_MODEL), f32)
    positions = jnp.broadcast_to(jnp.arange(SEQ, dtype=jnp.int32), (BATCH, SEQ))
    return {
        "x": x,
        "positions": positions,
        "mix_norm_g": gain(ks[1], (L, D_MODEL)),
        "w_in": nrm(ks[2], (L, D_MODEL, IN_TOTAL), D_MODEL ** -0.5),
        "q_norm_g": gain(ks[3], (L, Q_LORA)),
        "w_q_up": nrm(ks[4], (L, Q_LORA, B_HEADS * (B_NOPE + B_ROPE)), Q_LORA ** -0.5),
        "kv_norm_g": gain(ks[5], (L, KV_LORA)),
        "w_kv_up": nrm(ks[6], (L, KV_LORA, B_HEADS * (B_NOPE + B_V)), KV_LORA ** -0.5),
        "w_branch_a": nrm(ks[7], (L, A_WIDTH, D_MODEL), A_WIDTH ** -0.5),
        "w_branch_b": nrm(ks[8], (L, B_WIDTH, D_MODEL), B_WIDTH ** -0.5),
        "w_out": nrm(ks[9], (L, D_MODEL, D_MODEL), D_MODEL ** -0.5),
        "ffn_norm_g": gain(ks[10], (L, D_MODEL)),
        "w_peer_query": nrm(ks[11], (L, D_MODEL, PEER_HEADS * PEER_QDIM), D_MODEL ** -0.5),
        "peer_sub_keys_1": nrm(ks[12], (L, PEER_HEADS, PEER_NKEYS, PEER_HALF), PEER_HALF ** -0.5),
        "peer_sub_keys_2": nrm(ks[13], (L, PEER_HEADS, PEER_NKEYS, PEER_HALF), PEER_HALF ** -0.5),
        "peer_expert_u": nrm(ks[14], (L, PEER_EXPERTS, D_MODEL), D_MODEL ** -0.5),
        "peer_expert_v": nrm(ks[15], (L, PEER_EXPERTS, D_MODEL), (PEER_HEADS * PEER_TOPK) ** -0.5),
        "final_norm_g": gain(ks[16], (D_MODEL,)),
    }


def reference(x, positions, mix_norm_g, w_in, q_norm_g, w_q_up, kv_norm_g, w_kv_up,
              w_branch_a, w_branch_b, w_out, ffn_norm_g, w_peer_query, peer_sub_keys_1,
              peer_sub_keys_2, peer_expert_u, peer_expert_v, final_norm_g):
    B, S, _ = x.shape
    cos_a, sin_a = rope_tables(positions, A_HEAD_DIM, x.dtype)
    cos_b, sin_b = rope_tables(positions, B_ROPE, x.dtype)
    splits = [sum(IN_WIDTHS[:i + 1]) for i in range(len(IN_WIDTHS) - 1)]
    for layer in range(DEPTH):
        h = rms_norm(x, mix_norm_g[layer])
        proj = h @ w_in[layer]
        q_a, k_a, v_a, c_q, c_kv, k_pe, gate_a, gate_b = jnp.split(proj, splits, axis=-1)
        q_a = apply_rope(q_a.reshape(B, S, A_HEADS, A_HEAD_DIM), cos_a, sin_a)
        k_a = apply_rope(k_a.reshape(B, S, A_HEADS, A_HEAD_DIM), cos_a, sin_a)
        v_a = v_a.reshape(B, S, A_HEADS, A_HEAD_DIM)
        y_a = moba_attention(q_a, k_a, v_a)
        y_b = mla_attention(c_q, c_kv, k_pe, q_norm_g[layer], w_q_up[layer],
                            kv_norm_g[layer], w_kv_up[layer], cos_b, sin_b)
        merged = (jax.nn.sigmoid(gate_a) * (y_a @ w_branch_a[layer])
                  + jax.nn.sigmoid(gate_b) * (y_b @ w_branch_b[layer]))
        x = x + merged @ w_out[layer]
        x = x + peer_ffn(rms_norm(x, ffn_norm_g[layer]), w_peer_query[layer],
                         peer_sub_keys_1[layer], peer_sub_keys_2[layer],
                         peer_expert_u[layer], peer_expert_v[layer])
    return rms_norm(x, final_norm_g)
```

```python
import contextlib
import math
import numpy as np
import concourse.bass as bass
import concourse.mybir as mybir
from concourse.alu_op_type import AluOpType as ALU
from concourse.bass_utils import run_bass_kernel_spmd

F32 = mybir.dt.float32
BF16 = mybir.dt.bfloat16
I32 = mybir.dt.int32
U32 = mybir.dt.uint32
AF = mybir.ActivationFunctionType
AX = mybir.AxisListType

S_LEN = 4096
D = 1024
NT = S_LEN // 128
NG = S_LEN // 512
NCOL = 8 * 320 + 256 + 128 + 64 + 2048
OFF_CQ = 2560
OFF_CKV = 2816
OFF_KPE = 2944
OFF_GA = 3008
OFF_GB = 4032
EPS = 1e-6
THETA = 10000.0
BIG = 8192.0
PG = 256
NPG = S_LEN // PG


class Tok:
    __slots__ = ("name", "last_w", "readers")

    def __init__(self, name=""):
        self.name = name
        self.last_w = None
        self.readers = {}


class Sched:
    ENG = ("pe", "act", "dve", "pool", "sp")
    NDMA = 48

    def __init__(self, nc, stack):
        self.nc = nc
        self.q = {e: [] for e in self.ENG}
        self.cnt = {e: 0 for e in self.ENG}
        self.seen = {e: {} for e in self.ENG}
        self.sem = {e: stack.enter_context(nc.semaphore("c_" + e)) for e in self.ENG}
        self.dsem = [stack.enter_context(nc.semaphore("d%d" % i)) for i in range(self.NDMA)]
        self.ndma = 0
        self.last_dma = {}
        self.same_engine_sync = True
        self.nosync_engines = ("pe",)
        self.stack = stack

    def sb(self, name, shape, dt):
        return self.stack.enter_context(self.nc.sbuf_tensor(name, shape, dt))

    def ps(self, name, shape, dt=F32):
        return self.stack.enter_context(self.nc.psum_tensor(name, shape, dt))

    def _deps(self, eng, reads, writes):
        deps = []
        for t in reads:
            if t.last_w is not None:
                deps.append(t.last_w)
        for t in writes:
            if t.last_w is not None:
                deps.append(t.last_w)
            deps.extend(t.readers.values())
        waits = {}
        own = self.sem[eng]
        for (s, v) in deps:
            if s is own and (eng in self.nosync_engines or not self.same_engine_sync):
                continue
            k = id(s)
            if self.seen[eng].get(k, 0) >= v:
                continue
            if k not in waits or waits[k][1] < v:
                waits[k] = (s, v)
        for k, (s, v) in waits.items():
            self.seen[eng][k] = v
        return list(waits.values())

    def _mark(self, tok, reads, writes):
        for t in reads:
            k = id(tok[0])
            if k not in t.readers or t.readers[k][1] < tok[1]:
                t.readers[k] = tok
        for t in writes:
            t.last_w = tok
            t.readers = {}

    def op(self, eng, fn, reads=(), writes=()):
        waits = self._deps(eng, reads, writes)
        self.cnt[eng] += 1
        tok = (self.sem[eng], self.cnt[eng])
        self.q[eng].append((waits, fn, (self.sem[eng], 1)))
        self._mark(tok, reads, writes)
        return tok

    def dma(self, eng, fn, reads=(), writes=()):
        waits = self._deps(eng, reads, writes)
        i = self.ndma
        self.ndma += 1
        s = self.dsem[i % self.NDMA]
        k = i // self.NDMA
        if k > 0:
            need = 16 * k
            if self.seen[eng].get(id(s), 0) < need:
                waits.append((s, need))
                self.seen[eng][id(s)] = need
        tok = (s, 16 * (k + 1))
        self.q[eng].append((waits, fn, (s, 16)))
        self._mark(tok, reads, writes)
        self.last_dma[id(s)] = tok
        return tok

    def barrier(self):
        toks = [(self.sem[e], self.cnt[e]) for e in self.ENG if self.cnt[e] > 0]
        toks += list(self.last_dma.values())
        for e in self.ENG:
            waits = []
            for (s, v) in toks:
                if s is self.sem[e]:
                    continue
                if self.seen[e].get(id(s), 0) < v:
                    waits.append((s, v))
                    self.seen[e][id(s)] = v
            self.q[e].append((waits, None, None))

    def wait_all(self, eng, toks):
        waits = []
        for (s, v) in toks:
            if self.seen[eng].get(id(s), 0) < v:
                waits.append((s, v))
                self.seen[eng][id(s)] = v
        self.q[eng].append((waits, None, None))

    def emit(self):
        nc = self.nc
        hmap = {"pe": "tensor", "act": "scalar", "dve": "vector", "pool": "gpsimd", "sp": "sync"}
        with nc.Block() as block:
            for ename in self.ENG:
                items = self.q[ename]

                def body(e, items=items):
                    for (waits, fn, inc) in items:
                        for (s, v) in waits:
                            e.wait_ge(s, v)
                        if fn is not None:
                            ins = fn(e)
                            if inc is not None:
                                ins.then_inc(inc[0], inc[1])

                getattr(block, hmap[ename])(body)


def build_nc(dbg=False, stop_after=None, skip=()):
    nc = bass.Bass("TRN2", target_bir_lowering=False)

    def din(name, shape, dt=F32):
        return nc.dram_tensor(name, shape, dt, kind="ExternalInput").ap()

    x_d = din("x", [S_LEN, D])
    pos_d = din("pos", [1, S_LEN], I32)
    w_in_d = din("w_inr", [128, 8, NCOL])
    mixg_d = din("mixg", [128, 8])
    ffng_d = din("ffng", [128, 8])
    fing_d = din("fing", [1, D])
    qng_d = din("qng", [128, 2])
    kvng_d = din("kvng", [128, 1])
    wqu_d = din("wqu", [128, 8, 2, 2, 96])
    wkv_d = din("wkv", [128, 8, 128])
    wa_d = din("wa", [128, 4, D])
    wb_d = din("wb", [128, 4, D])
    wo_d = din("wo", [128, 8, D])
    wpq_d = din("wpq", [128, 8, 2048])
    keys_d = din("keysT", [128, 16, 128])
    ut_d = din("uT", [128, 128, 1024])
    v_d = din("vE", [128, 128, 1024])
    out_d = nc.dram_tensor("out", [S_LEN, D], F32, kind="ExternalOutput").ap()
    kind_scr = "ExternalOutput" if dbg else "Internal"
    x1_d = nc.dram_tensor("x1s", [S_LEN, D], F32, kind=kind_scr).ap()
    us_d = nc.dram_tensor("us", [128, 128, 1024], BF16, kind="Internal").ap()
    vs_d = nc.dram_tensor("vs", [128, 128, 1024], BF16, kind="Internal").ap()
    wpqs_d = nc.dram_tensor("wpqs", [128, 8, 2048], BF16, kind="Internal").ap()
    if dbg:
        ya_d = nc.dram_tensor("dbg_ya", [128, 4, S_LEN], BF16, kind="ExternalOutput").ap()
        yb_d = nc.dram_tensor("dbg_yb", [128, 4, S_LEN], BF16, kind="ExternalOutput").ap()
        pe_d = nc.dram_tensor("dbg_pe", [S_LEN, D], F32, kind="ExternalOutput").ap()
        misc_d = nc.dram_tensor("dbg_misc", [128, 9, 4096], BF16, kind="ExternalOutput").ap()
        x1a_d = nc.dram_tensor("dbg_x1a", [S_LEN, D], F32, kind="ExternalOutput").ap()

    with contextlib.ExitStack() as st:
        S = Sched(nc, st)
        ARENA_N = 32768 + 60416
        arena = S.sb("arena", [128, ARENA_N], BF16)

        def carve(off, n):
            return arena[:, off:off + n]

        hT = carve(0, 32768).rearrange("p (c t) -> p c t", c=8)
        O = 32768
        YT4 = carve(O, 16384).rearrange("p (c t) -> p c t", c=4)
        O2 = O + 16384
        ropeC = carve(O2, 4096)
        ropeS = carve(O2 + 4096, 4096)
        qaug = carve(O2 + 8192, 4096)
        kaug = carve(O2 + 12288, 4096)
        vaug = carve(O2 + 16384, 2112)[:, 0:NT * 65].rearrange("p (t f) -> p t f", t=NT)
        wh = [carve(O2 + 18496, 3584), carve(O2 + 18496 + 3584, 2560)]
        PTb = [carve(O2 + 18496 + 6144 + i * 512, 512) for i in range(3)]
        o3 = O2 + 18496 + 6144 + 1536
        cqT = carve(o3, 8192).rearrange("p (c t) -> p c t", c=2)
        ckvT = carve(o3 + 8192, 4096)
        o4 = o3 + 12288
        wqu_sb = [carve(o4 + i * 384, 384).rearrange("p (c s f) -> p c s f", c=2, s=2) for i in range(2)]
        wkv_sb = [carve(o4 + 768 + i * 128, 128) for i in range(2)]
        att_end = o4 + 1024
        assert att_end <= ARENA_N, att_end
        YT = carve(O2, 2048).rearrange("p (c t) -> p c t", c=4)
        ma = carve(O2 + 2048, 4096).rearrange("p (c t) -> p c t", c=8)
        wab = carve(O2 + 6144, 4096).rearrange("p (c f) -> p c f", c=4)
        wo_sb = carve(O2 + 10240, 8192).rearrange("p (c f) -> p c f", c=8)
        wg = [carve(O2 + 18432 + i * 1024, 1024).rearrange("p (c f) -> p c f", c=8) for i in range(2)]
        sgb = [carve(O2 + 20480 + i * 512, 512) for i in range(2)]
        Wbuf = carve(O, PG * 128).rearrange("p (t i) -> p t i", t=PG)
        p1 = O + PG * 128
        NUB = 4
        ub = [carve(p1 + i * 1024, 1024).rearrange("p (c j) -> p c j", c=8) for i in range(NUB)]
        vb = [carve(p1 + NUB * 1024 + i * 1024, 1024) for i in range(NUB)]
        p2 = p1 + 2 * NUB * 1024
        qTs = carve(p2, 16 * PG).rearrange("p (q t) -> p q t", q=16)
        p3 = p2 + 16 * PG
        keys_sb = carve(p3, 2048).rearrange("p (q n) -> p q n", q=16)
        p4 = p3 + 2048
        wpq_sb = carve(p4, 2048).rearrange("p (c f) -> p c f", c=8)
        p5 = p4 + 2048
        NAB = 4
        At = [carve(p5 + i * 128, 128) for i in range(NAB)]
        Bt = [carve(p5 + NAB * 128 + i * 128, 128) for i in range(NAB)]
        p6 = p5 + 2 * NAB * 128
        geb = [carve(p6 + i * PG, PG) for i in range(2)]
        wab2 = [carve(p6 + 2 * PG + i * PG, PG) for i in range(2)]
        p7 = p6 + 4 * PG

        def carve32(off, n, dt=F32):
            return carve(off, 2 * n).bitcast(dt)

        s_all = carve32(p7, 2048).rearrange("p (q n) -> p q n", q=16)
        eq = carve32(p7, 2048).rearrange("p (h k a) -> p h k a", h=8, k=16)
        xwP = carve32(p7, 1024)
        p7 += 4096
        cand1 = carve32(p7, 256)
        best = carve32(p7 + 512, 128).rearrange("p (a b) -> p a b", a=8)
        posu = carve32(p7 + 768, 128, U32).rearrange("p (a b) -> p a b", a=8)
        abu = carve32(p7 + 1024, 256, U32).rearrange("p (s a b) -> p s a b", s=2, a=8)
        abf = carve32(p7 + 1536, 256).rearrange("p (s a b) -> p s a b", s=2, a=8)
        GIJ = carve32(p7 + 2048, 384).rearrange("p (a b) -> p a b", a=3)
        GIJT = carve32(p7 + 2816, 384).rearrange("p (a b) -> p a b", a=3)
        ez = carve32(p7 + 3584, 128).rearrange("p (a b) -> p a b", a=8)
        zs = carve32(p7 + 3840, 16)
        GIJT2 = [GIJT, carve32(p7 + 3872, 384).rearrange("p (a b) -> p a b", a=3)]
        peer_end = p7 + 3872 + 768
        assert peer_end <= ARENA_N, peer_end

        identb = S.sb("identb", [128, 128], BF16)
        identf = S.sb("identf", [128, 128], F32)
        tri = S.sb("tri", [128, 128], BF16)
        onesb = S.sb("onesb", [128, 128], BF16)
        iof = S.sb("iof", [128, 128], F32)
        iob = S.sb("iob", [128, 128], BF16)
        onesf = S.sb("onesf", [128, 64], F32)
        ioi = S.sb("ioi", [128, 128], I32)
        xt = [S.sb("xt%d" % i, [128, D], F32) for i in range(2)]
        wkall = S.sb("wkall", [128, 4, 512], F32)
        wk = [wkall[:, i, :] for i in range(4)]
        xw = wkall[:, 0:2, :].rearrange("p a b -> p (a b)")
        fing = xw
        wk23 = wkall[:, 2:4, :].rearrange("p a b -> p (a b)")
        swk = wk23[:, 0:256]
        v12 = wk23[:, 256:512].rearrange("p (h s k) -> p h s k", h=8, s=2)
        i12u = wk23[:, 512:768].bitcast(U32).rearrange("p (h s k) -> p h s k", h=8, s=2)
        i12f = wk23[:, 768:1024].rearrange("p (h s k) -> p h s k", h=8, s=2)
        wki = S.sb("wki", [128, 512], I32)
        mixg = S.sb("mixg_sb", [128, 8], F32)
        ffng = S.sb("ffng_sb", [128, 8], F32)
        qng = S.sb("qng_sb", [128, 2], F32)
        kvng = S.sb("kvng_sb", [128, 1], F32)
        sm = S.sb("sm", [128, 64], F32)
        ksum = S.sb("ksum", [128, 16], F32)
        kmT = S.sb("kmT", [128, 16], BF16)
        padrow = S.sb("padrow", [128, 16, 16], F32)
        gm = S.sb("gm", [128, 16], F32)
        top8 = S.sb("top8", [128, 8], F32)
        biasq = S.sb("biasq", [128, 16], BF16)
        rec = S.sb("rec", [128, 8], F32)
        GIJb = S.sb("GIJb", [128, 3, 128], F32)

        pb = [S.ps("pb%d" % i, [128, 512], F32) for i in range(8)]
        pbT = [Tok("pb%d" % i) for i in range(8)]
        pb7h = pb[7][:].bitcast(BF16)

        T = {}

        def tk(name):
            if name not in T:
                T[name] = Tok(name)
            return T[name]

        hTt = [tk("hT%d" % g) for g in range(NG)]
        Yt = [tk("Y%d" % g) for g in range(NG)]
        qt_ = [tk("qaug%d" % g) for g in range(NG)]
        kt_ = [tk("kaug%d" % g) for g in range(NG)]
        vt_ = [tk("vaug%d" % g) for g in range(NG)]
        xtT = [tk("xt0"), tk("xt1")]
        wkT = [tk("wk%d" % i) for i in range(4)]

        def MM(out, lhsT, rhs, start, stop, reads, writes):
            S.op("pe", lambda e: e.matmul(out, lhsT=lhsT, rhs=rhs, start=start, stop=stop), reads, writes)

        def TR(out, in_, ident, reads, writes):
            S.op("pe", lambda e: e.transpose(out=out, in_=in_, identity=ident), reads, writes)

        def ACT(out, in_, func, reads, writes, scale=None, bias=None):
            kw = {}
            if scale is not None:
                kw["scale"] = scale
            if bias is not None:
                kw["bias"] = bias
            S.op("act", lambda e: e.activation(out=out, in_=in_, func=func, **kw), reads, writes)

        def TT(eng, out, in0, in1, op, reads, writes):
            S.op(eng, lambda e: e.tensor_tensor(out=out, in0=in0, in1=in1, op=op), reads, writes)

        def TS(eng, out, in0, s1, s2, op0, op1, reads, writes):
            if op1 is None:
                S.op(eng, lambda e: e.tensor_scalar(out=out, in0=in0, scalar1=s1, scalar2=None, op0=op0), reads, writes)
            else:
                S.op(eng, lambda e: e.tensor_scalar(out=out, in0=in0, scalar1=s1, scalar2=s2, op0=op0, op1=op1), reads, writes)

        def CP(eng, out, in_, reads, writes):
            if eng == "act":
                S.op("act", lambda e: e.copy(out=out, in_=in_), reads, writes)
            else:
                S.op(eng, lambda e: e.tensor_copy(out=out, in_=in_), reads, writes)

        def MS(eng, ap, val, writes):
            S.op(eng, lambda e: e.memset(ap, val), (), writes)

        def DMA(eng, out, in_, reads, writes):
            return S.dma(eng, lambda e: e.dma_start(out=out, in_=in_), reads, writes)

        bank_rr = [0]

        def next_bank(n=6):
            b = bank_rr[0] % n
            bank_rr[0] += 1
            return b

        stashT = [tk("stash_u"), tk("stash_v")]

        def stash_tables(lo, hi):
            for i in range(lo, hi):
                DMA("pool", us_d[i], ut_d[i], (), [stashT[0]])
                DMA("pool", vs_d[i], v_d[i], (), [stashT[1]])

        S.op("pool", lambda e: e.iota(ioi[:], pattern=[[1, 128]], base=0, channel_multiplier=0), (), [tk("ioi")])
        CP("dve", iof[:], ioi[:], [tk("ioi")], [tk("iof")])
        CP("dve", iob[:], ioi[:], [tk("ioi")], [tk("iob")])
        MS("dve", onesf[:], 1.0, [tk("onesf")])
        S.op("pool", lambda e: e.iota(wki[:, 0:128], pattern=[[1, 128]], base=0, channel_multiplier=-1), (), [tk("wki")])
        CP("dve", wk[0][:, 0:128], wki[:, 0:128], [tk("wki")], [wkT[0]])
        S.op("dve", lambda e: e.tensor_single_scalar(out=identb[:], in_=wk[0][:, 0:128], scalar=0.0, op=ALU.is_equal), [wkT[0]], [tk("identb")])
        S.op("dve", lambda e: e.tensor_single_scalar(out=identf[:], in_=wk[0][:, 0:128], scalar=0.0, op=ALU.is_equal), [wkT[0]], [tk("identf")])
        S.op("dve", lambda e: e.tensor_single_scalar(out=tri[:], in_=wk[0][:, 0:128], scalar=0.0, op=ALU.is_ge), [wkT[0]], [tk("tri")])
        MS("dve", onesb[:], 1.0, [tk("onesb")])
        S.op("pool", lambda e: e.iota(wki[:, 0:256], pattern=[[-1, 16], [1, 16]], base=0, channel_multiplier=0), [wkT[0]], [tk("wki")])
        CP("dve", wk[1][:, 0:256], wki[:, 0:256], [tk("wki")], [wkT[1]])
        TS("dve", padrow[:].rearrange("p a b -> p (a b)"), wk[1][:, 0:256], 0.0, -1e30, ALU.is_ge, ALU.mult, [wkT[1]], [tk("padrow")])
        DMA("sp", mixg[:], mixg_d[:, :], (), [tk("mixg")])
        DMA("sp", ffng[:], ffng_d[:, :], (), [tk("ffng")])
        DMA("sp", qng[:], qng_d[:, :], (), [tk("qng")])
        DMA("sp", kvng[:], kvng_d[:, :], (), [tk("kvng")])

        def build_rope(p0, n, half, ident_rows):
            tR = tk("rope")
            smT = tk("sm")
            pr = slice(p0, p0 + n)
            if ident_rows:
                MS("pool", ropeC[0:ident_rows, :], 1.0, [tR])
                MS("pool", ropeS[0:ident_rows, :], 0.0, [tR])
            S.op("pool", lambda e: e.iota(wki[pr, 0:1], pattern=[[0, 1]], base=0, channel_multiplier=1), [wkT[1]], [tk("wki")])
            CP("dve", sm[pr, 0:1], wki[pr, 0:1], [tk("wki")], [smT])
            TS("dve", sm[pr, 1:2], sm[pr, 0:1], float(half), None, ALU.is_ge, None, [smT], [smT])
            S.op("dve", lambda e: e.scalar_tensor_tensor(out=sm[pr, 2:3], in0=sm[pr, 1:2], scalar=-float(half), in1=sm[pr, 0:1], op0=ALU.mult, op1=ALU.add), [smT], [smT])
            ACT(sm[pr, 3:4], sm[pr, 2:3], AF.Exp, [smT], [smT], scale=-math.log(THETA) / half)
            TS("dve", sm[pr, 4:5], sm[pr, 1:2], 2.0, -1.0, ALU.mult, ALU.add, [smT], [smT])
            for cch in range(S_LEN // 512):
                cs = slice(cch * 512, (cch + 1) * 512)
                DMA("sp", wki[pr, :], pos_d[0:1, cs].to_broadcast([n, 512]), [wkT[0]], [tk("wki")])
                CP("dve", wk[0][pr, :], wki[pr, :], [tk("wki")], [wkT[0]])
                TS("dve", wk[0][pr, :], wk[0][pr, :], sm[pr, 3:4], None, ALU.mult, None, [wkT[0], smT], [wkT[0]])
                for which in range(2):
                    a = wk[1][pr, :]
                    if which == 0:
                        CP("dve", a, wk[0][pr, :], [wkT[0]], [wkT[1]])
                    else:
                        TS("dve", a, wk[0][pr, :], math.pi / 2, None, ALU.add, None, [wkT[0]], [wkT[1]])
                    TS("dve", wk[2][pr, :], a, 1.0 / (2 * math.pi), None, ALU.mult, None, [wkT[1]], [wkT[2]])
                    CP("dve", wki[pr, :], wk[2][pr, :], [wkT[2]], [tk("wki")])
                    CP("dve", wk[2][pr, :], wki[pr, :], [tk("wki")], [wkT[2]])
                    S.op("dve", lambda e, a=a: e.scalar_tensor_tensor(out=a, in0=wk[2][pr, :], scalar=-2 * math.pi, in1=a, op0=ALU.mult, op1=ALU.add), [wkT[1], wkT[2]], [wkT[1]])
                    TS("dve", wk[2][pr, :], a, math.pi, -2 * math.pi, ALU.is_gt, ALU.mult, [wkT[1]], [wkT[2]])
                    TT("dve", a, a, wk[2][pr, :], ALU.add, [wkT[1], wkT[2]], [wkT[1]])
                    TS("dve", wk[2][pr, :], a, -math.pi, 2 * math.pi, ALU.is_lt, ALU.mult, [wkT[1]], [wkT[2]])
                    TT("dve", a, a, wk[2][pr, :], ALU.add, [wkT[1], wkT[2]], [wkT[1]])
                    TS("dve", a, a, -math.pi, math.pi, ALU.max, ALU.min, [wkT[1]], [wkT[1]])
                    if which == 0:
                        ACT(wk[3][pr, :], a, AF.Sin, [wkT[1]], [wkT[3]])
                        TS("dve", ropeS[pr, cs], wk[3][pr, :], sm[pr, 4:5], None, ALU.mult, None, [wkT[3], smT], [tR])
                    else:
                        ACT(ropeC[pr, cs], a, AF.Sin, [wkT[1]], [tR])

        def norm_to_hT(src, srcT, tile_idx, gains, gT):
            g = tile_idx // 4
            xw_ = xw if tile_idx % 2 == 0 else wk23
            xwT = tk("xw%d" % (tile_idx % 2))
            sc = 8 + 4 * (tile_idx % 2)
            ACT(xw_[:], src, AF.Square, [srcT], [xwT])
            S.op("dve", lambda e, xw_=xw_, sc=sc: e.tensor_reduce(out=sm[:, sc:sc + 1], in_=xw_[:], axis=AX.X, op=ALU.add), [xwT], [tk("smA%d" % sc)])
            ACT(sm[:, sc + 1:sc + 2], sm[:, sc:sc + 1], AF.Sqrt, [tk("smA%d" % sc)], [tk("smB%d" % sc)], scale=1.0 / D, bias=EPS)
            S.op("dve", lambda e, sc=sc: e.reciprocal(out=sm[:, sc + 2:sc + 3], in_=sm[:, sc + 1:sc + 2]), [tk("smB%d" % sc)], [tk("smC%d" % sc)])
            TS("dve", xw_[:], src, sm[:, sc + 2:sc + 3], None, ALU.mult, None, [srcT, tk("smC%d" % sc)], [xwT])
            for half in range(2):
                b = next_bank()
                for c4 in range(4):
                    c = half * 4 + c4
                    TR(pb[b][:, c4 * 128:(c4 + 1) * 128], xw_[:, c * 128:(c + 1) * 128], identf[:], [xwT, tk("identf")], [pbT[b]])
                for c4 in range(4):
                    c = half * 4 + c4
                    ACT(hT[:, c, tile_idx * 128:(tile_idx + 1) * 128], pb[b][:, c4 * 128:(c4 + 1) * 128], AF.Copy,
                        [pbT[b], gT], [hTt[g]], scale=gains[:, c:c + 1])

        whT = [tk("wh0"), tk("wh1")]

        def load_wh_moba(h):
            DMA("pool", wh[h % 2][:, 0:2560].rearrange("p (c f) -> p c f", c=8), w_in_d[:, :, h * 320:(h + 1) * 320], (), [whT[h % 2]])

        S.barrier()
        load_wh_moba(0)
        DMA("pool", wpqs_d[:, :, :], wpq_d[:, :, :], (), [tk("wpqs")])
        stash_tables(0, 16)

        DMA("sp", xt[0][:], x_d[0:128, :], (), [xtT[0]])
        for t in range(NT):
            if t + 1 < NT:
                DMA("sp", xt[(t + 1) % 2][:], x_d[(t + 1) * 128:(t + 2) * 128, :], (), [xtT[(t + 1) % 2]])
            norm_to_hT(xt[t % 2][:], xtT[t % 2], t, mixg, tk("mixg"))

        S.barrier()

        def attention(h, K, scale):
            hp = (h % 2) * 64
            pairs = [(Q, j) for Q in range(NG) for j in range(4 * Q + 4)]
            sbanks = [0, 1, 5, 6]
            SKEW = 3

            def emit_S(k):
                Q, j = pairs[k]
                r = j - 4 * Q
                q0 = 128 * max(r, 0)
                sb_ = sbanks[k % 4]
                MM(pb[sb_][:, q0:512], kaug[0:K, j * 128:(j + 1) * 128], qaug[0:K, Q * 512 + q0:(Q + 1) * 512], True, True,
                   [kt_[j // 4], qt_[Q]], [pbT[sb_]])

            def emit_norm_pe(Q):
                MM(pb[4][0:64, :], onesf[64:65, :], wk[3][64:65, :], True, True, [tk("onesf"), wkT[3]], [pbT[4]])
                S.op("dve", lambda e: e.reciprocal(out=wk[2][0:64, :], in_=pb[4][0:64, :]), [pbT[4]], [wkT[2]])
                ab = 2 + (Q % 2)
                TT("dve", YT4[hp:hp + 64, h // 2, Q * 512:(Q + 1) * 512], pb[ab][0:64, :], wk[2][0:64, :], ALU.mult, [pbT[ab], wkT[2]], [Yt[Q]])

            for k in range(min(SKEW, len(pairs))):
                emit_S(k)
            deferred = {}
            for k in range(len(pairs)):
                Q, j = pairs[k]
                nj = 4 * Q + 4
                ab = 2 + (Q % 2)
                r = j - 4 * Q
                q0 = 128 * max(r, 0)
                sb_ = sbanks[k % 4]
                pt = PTb[k % 3]
                ptT = tk("PT%d" % (k % 3))
                if k + SKEW < len(pairs):
                    emit_S(k + SKEW)
                ACT(pt[:, q0:512], pb[sb_][:, q0:512], AF.Exp, [pbT[sb_]], [ptT], scale=scale)
                if r >= 0:
                    TT("dve", pt[:, q0:q0 + 128], pt[:, q0:q0 + 128], tri[:], ALU.mult, [ptT, tk("tri")], [ptT])
                MM(pb[ab][0:65, q0:512], vaug[:, j, :], pt[:, q0:512], j == 0, j == nj - 1, [ptT, vt_[j // 4]], [pbT[ab]])
                if k in deferred:
                    emit_norm_pe(deferred.pop(k))
                if j == nj - 1:
                    CP("act", wk[3][64:65, :], pb[ab][64:65, :], [pbT[ab]], [wkT[3]])
                    if k + 2 < len(pairs):
                        deferred[k + 2] = Q
                    else:
                        emit_norm_pe(Q)

        def rope_combine(rows, cs_tok, pq, pqs, out_ap, outT, want_f32=None, s_off=0):
            TT("dve", wk[0][0:rows, :], pb[pqs][s_off:s_off + rows, :], ropeS[s_off:s_off + rows, cs_tok], ALU.mult, [pbT[pqs], tk("rope")], [wkT[0]])
            TT("dve", wk[1][0:rows, :], pb[pq][0:rows, :], ropeC[0:rows, cs_tok], ALU.mult, [pbT[pq], tk("rope")], [wkT[1]])
            if want_f32 is None:
                TT("pool", out_ap, wk[0][0:rows, :], wk[1][0:rows, :], ALU.add, [wkT[0], wkT[1]], [outT])
            else:
                TT("pool", want_f32, wk[0][0:rows, :], wk[1][0:rows, :], ALU.add, [wkT[0], wkT[1]], [wkT[2]])
                CP("act", out_ap, want_f32, [wkT[2]], [outT])

        build_rope(0, 64, 32, 0)
        CP("act", ropeS[64:128, :], ropeS[0:64, :], [tk("rope")], [tk("rope")])
        S.op("pool", lambda e: e.iota(wki[64:80, :], pattern=[[1, 2], [0, 256]], base=0, channel_multiplier=-1), [wkT[0]], [tk("wki")])
        for cch in range(8):
            CP("dve", wk[0][64:80, :], wki[64:80, :], [tk("wki")], [wkT[0]])
            TS("dve", kaug[64:80, cch * 512:(cch + 1) * 512], wk[0][64:80, :], float(-2 * cch), None, ALU.is_equal, None, [wkT[0]], [kt_[cch]])
        MS("pool", qaug[64:80, :], 0.0, [qt_[g] for g in range(NG)])
        MS("pool", vaug[:, :, 64:65], 1.0, [vt_[g] for g in range(NG)])

        for h in range(8):
            if "moba" in skip:
                stash_tables(16 + h * 14, 16 + (h + 1) * 14)
                continue
            w3 = wh[h % 2][:, 0:2560].rearrange("p (c f) -> p c f", c=8)
            wT_ = whT[h % 2]
            if h + 1 < 8:
                load_wh_moba(h + 1)
            stash_tables(16 + h * 14, 16 + (h + 1) * 14)
            for g in range(NG):
                cs = slice(g * 512, (g + 1) * 512)
                bq, bk, bv = (0, 1, 2) if g % 2 == 0 else (3, 4, 5)
                for c in range(8):
                    MM(pb[bq][:, :], w3[:, c, 0:128], hT[:, c, cs], c == 0, c == 7, [wT_, hTt[g]], [pbT[bq]])
                for c in range(8):
                    MM(pb[bk][:, :], w3[:, c, 128:256], hT[:, c, cs], c == 0, c == 7, [wT_, hTt[g]], [pbT[bk]])
                for tt in range(4):
                    for c in range(8):
                        MM(pb[bv][:, tt * 64:(tt + 1) * 64], hT[:, c, g * 512 + tt * 128:g * 512 + (tt + 1) * 128], w3[:, c, 256:320],
                           c == 0, c == 7, [wT_, hTt[g]], [pbT[bv]])
                rope_combine(64, cs, bq, bq, qaug[0:64, cs], qt_[g], s_off=64)
                rope_combine(64, cs, bk, bk, kaug[0:64, cs], kt_[g], want_f32=wk[2][0:64, :], s_off=64)
                S.op("dve", lambda e, g=g: e.tensor_reduce(out=ksum[0:64, 2 * g:2 * g + 2], in_=wk[2][0:64, :].rearrange("p (b k) -> p b k", b=2),
                                                      axis=AX.X, op=ALU.add), [wkT[2]], [tk("ksum")])
                CP("act", vaug[:, g * 4:(g + 1) * 4, 0:64], pb[bv][:, 0:256].rearrange("p (t f) -> p t f", t=4), [pbT[bv]], [vt_[g]])
            S.op("act", lambda e: e.mul(out=kmT[0:64, :], in_=ksum[0:64, :], mul=1.0 / 256), [tk("ksum")], [tk("kmT")])
            for qt in range(8, NT):
                blk = qt // 2
                g = qt // 4
                MM(pb[6][:, 0:16], qaug[0:64, qt * 128:(qt + 1) * 128], kmT[0:64, :], True, True, [qt_[g], tk("kmT")], [pbT[6]])
                TT("dve", gm[:], pb[6][:, 0:16], padrow[:, blk, :], ALU.add, [pbT[6], tk("padrow")], [tk("gm")])
                S.op("dve", lambda e: e.max(out=top8[:], in_=gm[:]), [tk("gm")], [tk("top8")])
                MS("pool", biasq[:], 0.0, [tk("biasq")])
                TS("dve", biasq[:, 0:blk], gm[:, 0:blk], top8[:, 2:3], -BIG, ALU.is_lt, ALU.mult, [tk("gm"), tk("top8")], [tk("biasq")])
                TR(pb7h[64:80, 0:128], biasq[:], identb[:], [tk("biasq"), tk("identb")], [pbT[7]])
                CP("act", qaug[64:80, qt * 128:(qt + 1) * 128], pb7h[64:80, 0:128], [pbT[7]], [qt_[g]])
            attention(h, 80, 1.0 / 8.0)

        if dbg:
            DMA("sp", ya_d[:, :, :], YT4[:, :, :], Yt, [tk("dbg_ya")])

        def branch(which):
            wsrc = wa_d if which == 0 else wb_d
            goff = OFF_GA if which == 0 else OFF_GB
            S.barrier()
            DMA("pool", wab[:, :, :], wsrc[:, :, :], (), [tk("wab")])
            DMA("pool", wo_sb[:, :, :], wo_d[:, :, :], (), [tk("wo")])
            wgT = [tk("wg0"), tk("wg1")]
            sgT = [tk("sg0"), tk("sg1")]
            base_d = x_d if which == 0 else x1_d
            DMA("pool", wg[0][:, :, :], w_in_d[:, :, goff:goff + 128], (), [wgT[0]])
            for g in range(NG):
                cs = slice(g * 512, (g + 1) * 512)
                for oc in range(8):
                    k_ = g * 8 + oc
                    if k_ + 1 < NG * 8:
                        oc2 = (oc + 1) % 8
                        DMA("pool", wg[(k_ + 1) % 2][:, :, :], w_in_d[:, :, goff + oc2 * 128:goff + (oc2 + 1) * 128], (), [wgT[(k_ + 1) % 2]])
                    bb = next_bank(2)
                    bg = 2 + next_bank(2) % 2
                    for kc in range(4):
                        MM(pb[bb][:, :], wab[:, kc, oc * 128:(oc + 1) * 128], YT4[:, kc, cs], kc == 0, kc == 3, [tk("wab"), Yt[g]], [pbT[bb]])
                    for c in range(8):
                        MM(pb[bg][:, :], wg[k_ % 2][:, c, :], hT[:, c, cs], c == 0, c == 7, [wgT[k_ % 2], hTt[g]], [pbT[bg]])
                    ACT(sgb[k_ % 2][:, :], pb[bg][:, :], AF.Sigmoid, [pbT[bg]], [sgT[k_ % 2]])
                    TT("dve", ma[:, oc, :], pb[bb][:, :], sgb[k_ % 2][:, :], ALU.mult, [pbT[bb], sgT[k_ % 2]], [tk("ma")])
                for tt in range(4):
                    ti = g * 4 + tt
                    xb = xt[ti % 2]
                    xbT = xtT[ti % 2]
                    DMA("sp", xb[:], base_d[ti * 128:(ti + 1) * 128, :], [] if which == 0 else [tk("x1s%d" % ti)], [xbT])
                    for dh in range(2):
                        bo = 4 + dh
                        for oc in range(8):
                            MM(pb[bo][:, :], ma[:, oc, tt * 128:(tt + 1) * 128], wo_sb[:, oc, dh * 512:(dh + 1) * 512], oc == 0, oc == 7,
                               [tk("ma"), tk("wo")], [pbT[bo]])
                        TT("dve", xb[:, dh * 512:(dh + 1) * 512], pb[bo][:, :], xb[:, dh * 512:(dh + 1) * 512], ALU.add, [pbT[bo], xbT], [xbT])
                    DMA("sp", x1_d[ti * 128:(ti + 1) * 128, :], xb[:], [xbT], [tk("x1s%d" % ti)])
                    if which == 1:
                        norm_to_hT(xb[:], xbT, ti, ffng, tk("ffng"))

        branch(0)
        if dbg:
            S.barrier()
            DMA("sp", x1a_d[:, :], x1_d[:, :], [tk("x1s%d" % i) for i in range(NT)], [tk("dbg_x1a")])
        if stop_after == "branch_a":
            pass

        def mla():
            S.barrier()
            build_rope(64, 32, 16, 64)
            MS("pool", vaug[:, :, 64:65], 1.0, [vt_[g] for g in range(NG)])
            wm = wh[0][:, 0:3584].rearrange("p (c f) -> p c f", c=8)
            DMA("pool", wm, w_in_d[:, :, OFF_CQ:OFF_CQ + 448], (), [whT[0]])
            wquT = [tk("wqu0"), tk("wqu1")]
            wkvT = [tk("wkv0"), tk("wkv1")]

            def load_head(h):
                DMA("pool", wqu_sb[h % 2][:, :, :, :], wqu_d[:, h, :, :, :], (), [wquT[h % 2]])
                DMA("pool", wkv_sb[h % 2][:, :], wkv_d[:, h, :], (), [wkvT[h % 2]])

            load_head(0)
            for g in range(NG):
                cs = slice(g * 512, (g + 1) * 512)
                specs = [(0, 0, 128), (1, 128, 256), (2, 256, 384)]
                for (ci, lo, hi) in specs:
                    b = ci
                    for c in range(8):
                        MM(pb[b][:, :], wm[:, c, lo:hi], hT[:, c, cs], c == 0, c == 7, [whT[0], hTt[g]], [pbT[b]])
                sq = [PTb[0], PTb[1], PTb[2]]
                for ci in range(3):
                    ACT(sq[ci][:, :], pb[ci][:, :], AF.Square, [pbT[ci]], [tk("PT%d" % ci)])
                MM(pb[3][:, :], onesb[:], sq[0][:, :], True, False, [tk("onesb"), tk("PT0")], [pbT[3]])
                MM(pb[3][:, :], onesb[:], sq[1][:, :], False, True, [tk("onesb"), tk("PT1")], [pbT[3]])
                MM(pb[4][:, :], onesb[:], sq[2][:, :], True, True, [tk("onesb"), tk("PT2")], [pbT[4]])
                ACT(wk[0][:, :], pb[3][:, :], AF.Sqrt, [pbT[3]], [wkT[0]], scale=1.0 / 256, bias=EPS)
                S.op("dve", lambda e: e.reciprocal(out=wk[0][:, :], in_=wk[0][:, :]), [wkT[0]], [wkT[0]])
                ACT(wk[1][:, :], pb[4][:, :], AF.Sqrt, [pbT[4]], [wkT[1]], scale=1.0 / 128, bias=EPS)
                S.op("dve", lambda e: e.reciprocal(out=wk[1][:, :], in_=wk[1][:, :]), [wkT[1]], [wkT[1]])
                for cc in range(2):
                    S.op("dve", lambda e, cc=cc, cs=cs: e.scalar_tensor_tensor(out=cqT[:, cc, cs], in0=pb[cc][:, :], scalar=qng[:, cc:cc + 1], in1=wk[0][:, :],
                                                                   op0=ALU.mult, op1=ALU.mult), [pbT[cc], wkT[0], tk("qng")], [tk("cqT%d" % g)])
                S.op("dve", lambda e, cs=cs: e.scalar_tensor_tensor(out=ckvT[:, cs], in0=pb[2][:, :], scalar=kvng[:, 0:1], in1=wk[1][:, :],
                                                        op0=ALU.mult, op1=ALU.mult), [pbT[2], wkT[1], tk("kvng")], [tk("ckvT%d" % g)])
                for c in range(8):
                    MM(pb[5][64:96, :], wm[:, c, 384:416], hT[:, c, cs], c == 0, c == 7, [whT[0], hTt[g]], [pbT[5]])
                for c in range(8):
                    MM(pb[6][64:96, :], wm[:, c, 416:448], hT[:, c, cs], c == 0, c == 7, [whT[0], hTt[g]], [pbT[6]])
                TT("dve", wk[2][64:96, :], pb[6][64:96, :], ropeS[64:96, cs], ALU.mult, [pbT[6], tk("rope")], [wkT[2]])
                TT("dve", wk[3][64:96, :], pb[5][64:96, :], ropeC[64:96, cs], ALU.mult, [pbT[5], tk("rope")], [wkT[3]])
                TT("pool", kaug[64:96, cs], wk[2][64:96, :], wk[3][64:96, :], ALU.add, [wkT[2], wkT[3]], [kt_[g]])
            for h in range(8):
                if h + 1 < 8:
                    load_head(h + 1)
                wq_ = wqu_sb[h % 2]
                wv_ = wkv_sb[h % 2]
                for g in range(NG):
                    cs = slice(g * 512, (g + 1) * 512)
                    for cc in range(2):
                        MM(pb[0][0:96, :], wq_[:, cc, 0, :], cqT[:, cc, cs], cc == 0, cc == 1, [wquT[h % 2], tk("cqT%d" % g)], [pbT[0]])
                    for cc in range(2):
                        MM(pb[1][0:96, :], wq_[:, cc, 1, :], cqT[:, cc, cs], cc == 0, cc == 1, [wquT[h % 2], tk("cqT%d" % g)], [pbT[1]])
                    MM(pb[2][0:64, :], wv_[:, 0:64], ckvT[:, cs], True, True, [wkvT[h % 2], tk("ckvT%d" % g)], [pbT[2]])
                    for tt in range(4):
                        MM(pb[3][:, tt * 64:(tt + 1) * 64], ckvT[:, g * 512 + tt * 128:g * 512 + (tt + 1) * 128], wv_[:, 64:128], True, True,
                           [wkvT[h % 2], tk("ckvT%d" % g)], [pbT[3]])
                    rope_combine(96, cs, 0, 1, qaug[0:96, cs], qt_[g])
                    CP("act", kaug[0:64, cs], pb[2][0:64, :], [pbT[2]], [kt_[g]])
                    CP("act", vaug[:, g * 4:(g + 1) * 4, 0:64], pb[3][:, 0:256].rearrange("p (t f) -> p t f", t=4), [pbT[3]], [vt_[g]])
                attention(h, 96, 1.0 / math.sqrt(96.0))

        mla()
        if dbg:
            S.barrier()
            for i_, ap_ in enumerate([qaug, kaug, ropeC, ropeS, ckvT, cqT[:, 0, :], cqT[:, 1, :]]):
                DMA("sp", misc_d[:, i_, :], ap_, [], [tk("dbg_misc")])
            DMA("sp", misc_d[:, 7, 0:2080], vaug.rearrange("p t f -> p (t f)"), [], [tk("dbg_misc")])
            DMA("sp", yb_d[:, :, :], YT4[:, :, :], Yt, [tk("dbg_yb")])
        branch(1)

        S.barrier()
        if "peer" in skip:
            NPG_run = 0
        else:
            NPG_run = NPG
        DMA("pool", keys_sb[:, :, :], keys_d[:, :, :], (), [tk("keys")])
        DMA("sp", fing, fing_d.to_broadcast([128, D]), (), [tk("fing")])
        ubT = [tk("ub%d" % i) for i in range(NUB)]
        vbT = [tk("vb%d" % i) for i in range(NUB)]
        AtT = [tk("At%d" % i) for i in range(NAB)]
        BtT = [tk("Bt%d" % i) for i in range(NAB)]
        geT = [tk("ge0"), tk("ge1")]
        wa2T = [tk("wa20"), tk("wa21")]
        nload = [0]

        def load_uv(i):
            k = nload[0] % NUB
            nload[0] += 1
            DMA("sp", ub[k][:, :, :], us_d[i].rearrange("p (c j) -> p c j", c=8), [stashT[0]], [ubT[k]])
            DMA("sp", vb[k][:, :], vs_d[i], [stashT[1]], [vbT[k]])
            return k

        def bg_gen(G):
            gcs = slice(G * PG, (G + 1) * PG)
            g8 = G // 2
            wpT = [tk("wpq0"), tk("wpq1")]
            DMA("sp", wpq_sb[:, :, 0:128], wpqs_d[:, :, 0:128], [tk("wpqs")], [wpT[0]])
            for pq in range(16):
                hb = pq % 2
                if pq + 1 < 16:
                    DMA("sp", wpq_sb[:, :, (1 - hb) * 128:(2 - hb) * 128], wpqs_d[:, :, (pq + 1) * 128:(pq + 2) * 128], [tk("wpqs")], [wpT[1 - hb]])
                for c in range(8):
                    MM(pb[7][:, hb * PG:(hb + 1) * PG], wpq_sb[:, c, hb * 128:(hb + 1) * 128], hT[:, c, gcs], c == 0, c == 7,
                       [wpT[hb], hTt[g8]], [pbT[7]])
                CP("act", qTs[:, pq, :], pb[7][:, hb * PG:(hb + 1) * PG], [pbT[7]], [tk("qTs")])
                yield
            for tt in range(PG // 128):
                GIJ_ = GIJ if tt == 0 else GIJb
                gijT = tk("GIJ%d" % tt)
                if tt > 0:
                    for _ in range(12):
                        yield
                for b4 in range(4):
                    for q4 in range(4):
                        pq = b4 * 4 + q4
                        MM(pb[7][:, q4 * 128:(q4 + 1) * 128], qTs[:, pq, tt * 128:(tt + 1) * 128], keys_sb[:, pq, :], True, True,
                           [tk("qTs"), tk("keys")], [pbT[7]])
                    CP("act", s_all[:, b4 * 4:(b4 + 1) * 4, :], pb[7][:, :].rearrange("p (q n) -> p q n", q=4), [pbT[7]], [tk("s_all")])
                    yield
                for h in range(8):
                    for half in range(2):
                        src = s_all[:, 2 * h + half, :]
                        vv = v12[:, h, half, :]
                        iu = i12u[:, h, half, :]
                        S.op("dve", lambda e, src=src, vv=vv: e.max(out=vv[:, 0:8], in_=src), [tk("s_all")], [tk("v12")])
                        S.op("dve", lambda e, src=src, vv=vv, iu=iu: e.max_index(out=iu[:, 0:8], in_max=vv[:, 0:8], in_values=src), [tk("s_all"), tk("v12")], [tk("i12u")])
                        S.op("dve", lambda e, src=src, vv=vv: e.match_replace(out=swk[:, 0:128], in_to_replace=vv[:, 0:8], in_values=src, imm_value=-1e30),
                             [tk("s_all"), tk("v12")], [tk("swk")])
                        S.op("dve", lambda e, vv=vv: e.max(out=vv[:, 8:16], in_=swk[:, 0:128]), [tk("swk")], [tk("v12")])
                        S.op("dve", lambda e, vv=vv, iu=iu: e.max_index(out=iu[:, 8:16], in_max=vv[:, 8:16], in_values=swk[:, 0:128]), [tk("swk"), tk("v12")], [tk("i12u")])
                        yield
                CP("dve", i12f[:].rearrange("p a b c -> p (a b c)"), i12u[:].rearrange("p a b c -> p (a b c)"), [tk("i12u")], [tk("i12f")])
                for h in range(8):
                    S.op("dve", lambda e, h=h: e.tensor_tensor(out=cand1.rearrange("p (a b) -> p a b", a=16),
                                                           in0=v12[:, h, 0, :].unsqueeze(2).to_broadcast([128, 16, 16]),
                                                           in1=v12[:, h, 1, :].unsqueeze(1).to_broadcast([128, 16, 16]), op=ALU.add),
                         [tk("v12")], [tk("cand")])
                    src = cand1
                    bv_ = best[:, h, :]
                    pu = posu[:, h, :]
                    S.op("dve", lambda e, src=src, bv_=bv_: e.max(out=bv_[:, 0:8], in_=src), [tk("cand")], [tk("best")])
                    S.op("dve", lambda e, src=src, bv_=bv_, pu=pu: e.max_index(out=pu[:, 0:8], in_max=bv_[:, 0:8], in_values=src), [tk("cand"), tk("best")], [tk("posu")])
                    S.op("dve", lambda e, src=src, bv_=bv_: e.match_replace(out=swk[:, 0:256], in_to_replace=bv_[:, 0:8], in_values=src, imm_value=-1e30),
                         [tk("cand"), tk("best")], [tk("swk")])
                    S.op("dve", lambda e, bv_=bv_: e.max(out=bv_[:, 8:16], in_=swk[:, 0:256]), [tk("swk")], [tk("best")])
                    S.op("dve", lambda e, bv_=bv_, pu=pu: e.max_index(out=pu[:, 8:16], in_max=bv_[:, 8:16], in_values=swk[:, 0:256]), [tk("swk"), tk("best")], [tk("posu")])
                    yield
                pf = posu[:].rearrange("p a b -> p (a b)")
                S.op("dve", lambda e, pf=pf: e.tensor_single_scalar(out=abu[:, 0, :, :].rearrange("p a b -> p (a b)"), in_=pf, scalar=4, op=ALU.logical_shift_right), [tk("posu")], [tk("abu")])
                S.op("dve", lambda e, pf=pf: e.tensor_single_scalar(out=abu[:, 1, :, :].rearrange("p a b -> p (a b)"), in_=pf, scalar=15, op=ALU.bitwise_and), [tk("posu")], [tk("abu")])
                CP("dve", abf[:].rearrange("p a b c -> p (a b c)"), abu[:].rearrange("p a b c -> p (a b c)"), [tk("abu")], [tk("abf")])
                yield
                TT("dve", ez[:], best[:], best[:, :, 0:1].to_broadcast([128, 8, 16]), ALU.subtract, [tk("best")], [tk("ez")])
                for half in range(2):
                    for h in range(8):
                        S.op("dve", lambda e, h=h, half=half: e.tensor_tensor(out=eq[:, h, :, :], in0=abf[:, half, h, :].unsqueeze(2).to_broadcast([128, 16, 16]),
                                                                          in1=iof[:, 0:16].unsqueeze(1).to_broadcast([128, 16, 16]), op=ALU.is_equal),
                             [tk("abf"), tk("iof")], [tk("s_all")])
                        S.op("dve", lambda e, h=h, half=half: e.tensor_tensor(out=eq[:, h, :, :], in0=eq[:, h, :, :],
                                                                          in1=i12f[:, h, half, :].unsqueeze(1).to_broadcast([128, 16, 16]), op=ALU.mult),
                             [tk("s_all"), tk("i12f")], [tk("s_all")])
                        if h % 2 == 1:
                            yield
                    S.op("dve", lambda e, half=half, GIJ_=GIJ_: e.tensor_reduce(out=GIJ_[:, 1 + half, :], in_=eq[:].rearrange("p h k a -> p (h k) a"), axis=AX.X, op=ALU.add),
                         [tk("s_all")], [gijT])
                ACT(ez[:].rearrange("p a b -> p (a b)"), ez[:].rearrange("p a b -> p (a b)"), AF.Exp, [tk("ez")], [tk("ez")])
                S.op("dve", lambda e: e.tensor_reduce(out=zs[:, 0:8], in_=ez[:], axis=AX.X, op=ALU.add), [tk("ez")], [tk("zs")])
                S.op("dve", lambda e: e.reciprocal(out=zs[:, 8:16], in_=zs[:, 0:8]), [tk("zs")], [tk("zs")])
                TT("dve", GIJ_[:, 0, :].rearrange("p (h k) -> p h k", h=8), ez[:], zs[:, 8:16].unsqueeze(2).to_broadcast([128, 8, 16]), ALU.mult,
                   [tk("ez"), tk("zs")], [gijT])
                yield
            for _ in range(6):
                yield
            for tt in range(PG // 128):
                GIJ_ = GIJ if tt == 0 else GIJb
                gijT = tk("GIJ%d" % tt)
                for q in range(3):
                    TR(pb[7][:, q * 128:(q + 1) * 128], GIJ_[:, q, :], identf[:], [gijT, tk("identf")], [pbT[7]])
                CP("act", GIJT2[tt][:].rearrange("p a b -> p (a b)"), pb[7][:, 0:384], [pbT[7]], [tk("GIJT%d" % tt)])
                yield

        def scatter(G):
            for tt in range(PG // 128):
                gt = GIJT2[tt]
                gtT = tk("GIJT%d" % tt)
                for t4 in range(32):
                    b = next_bank(2)
                    for u in range(4):
                        t = t4 * 4 + u
                        ka = (t) % NAB
                        TS("dve", At[ka][:, :], iob[:, :], gt[:, 1, t:t + 1], gt[:, 0, t:t + 1], ALU.is_equal, ALU.mult,
                           [tk("iob"), gtT], [AtT[ka]])
                        TS("dve", Bt[ka][:, :], iob[:, :], gt[:, 2, t:t + 1], None, ALU.is_equal, None, [tk("iob"), gtT], [BtT[ka]])
                        MM(pb[b][:, u * 128:(u + 1) * 128], Bt[ka][:, :], At[ka][:, :], True, True, [AtT[ka], BtT[ka]], [pbT[b]])
                    tg = tt * 128 + t4 * 4
                    CP("act", Wbuf[:, tg:tg + 4, :], pb[b][:, :].rearrange("p (t i) -> p t i", t=4), [pbT[b]], [tk("Wbuf")])

        def dense(G, bg):
            gcs = slice(G * PG, (G + 1) * PG)
            g8 = G // 2
            pend = [load_uv(0), load_uv(1), load_uv(2)]
            abank = [0, 1, 6]

            def emit_u(i):
                k = pend[i]
                b = abank[i % 3]
                for c in range(8):
                    MM(pb[b][:, 0:PG], ub[k][:, c, :], hT[:, c, gcs], c == 0, c == 7, [ubT[k], hTt[g8]], [pbT[b]])

            emit_u(0)
            emit_u(1)
            for i in range(128):
                if i + 3 < 128:
                    pend.append(load_uv(i + 3))
                k = pend[i]
                b = abank[i % 3]
                b2 = i % 2
                if i + 2 < 128:
                    emit_u(i + 2)
                ACT(geb[b2][:, :], pb[b][:, 0:PG], AF.Gelu, [pbT[b]], [geT[b2]])
                TT("pool", wab2[b2][:, :], geb[b2][:, :], Wbuf[:, :, i], ALU.mult, [geT[b2], tk("Wbuf")], [wa2T[b2]])
                for tt in range(PG // 128):
                    for dh in range(2):
                        bo = 2 + tt * 2 + dh
                        MM(pb[bo][:, :], wab2[b2][:, tt * 128:(tt + 1) * 128], vb[k][:, dh * 512:(dh + 1) * 512], i == 0, i == 127,
                           [wa2T[b2], vbT[k]], [pbT[bo]])
                if bg is not None and i >= 2:
                    next(bg, None)
            if bg is not None:
                for _ in bg:
                    pass

        def final(G):
            for tt in range(PG // 128):
                ti = G * (PG // 128) + tt
                xb = xt[ti % 2]
                xbT = xtT[ti % 2]
                DMA("sp", xb[:], x1_d[ti * 128:(ti + 1) * 128, :], [tk("x1s%d" % ti)], [xbT])
                if dbg:
                    for dh in range(2):
                        CP("dve", xwP[:, dh * 512:(dh + 1) * 512], pb[2 + tt * 2 + dh][:, :], [pbT[2 + tt * 2 + dh]], [tk("s_all")])
                    DMA("sp", pe_d[ti * 128:(ti + 1) * 128, :], xwP[:], [tk("s_all")], [tk("dbg_pe")])
                for dh in range(2):
                    bo = 2 + tt * 2 + dh
                    TT("dve", xb[:, dh * 512:(dh + 1) * 512], pb[bo][:, :], xb[:, dh * 512:(dh + 1) * 512], ALU.add, [pbT[bo], xbT], [xbT])
                ACT(xwP[:], xb[:], AF.Square, [xbT], [tk("s_all")])
                S.op("dve", lambda e: e.tensor_reduce(out=sm[:, 8:9], in_=xwP[:], axis=AX.X, op=ALU.add), [tk("s_all")], [tk("sm8")])
                ACT(sm[:, 9:10], sm[:, 8:9], AF.Sqrt, [tk("sm8")], [tk("sm9")], scale=1.0 / D, bias=EPS)
                S.op("dve", lambda e: e.reciprocal(out=sm[:, 10:11], in_=sm[:, 9:10]), [tk("sm9")], [tk("sm10")])
                S.op("dve", lambda e, xb=xb: e.scalar_tensor_tensor(out=xb[:], in0=xb[:], scalar=sm[:, 10:11], in1=fing, op0=ALU.mult, op1=ALU.mult),
                     [xbT, tk("sm10"), tk("fing")], [xbT])
                DMA("sp", out_d[ti * 128:(ti + 1) * 128, :], xb[:], [xbT], [tk("out")])

        if NPG_run:
            for _ in bg_gen(0):
                pass
        for G in range(NPG_run):
            scatter(G)
            dense(G, bg_gen(G + 1) if G + 1 < NPG_run else None)
            final(G)

        S.barrier()
        S.emit()
    return nc


def _prep_inputs(inputs):
    f = np.float32
    w_in = np.asarray(inputs["w_in"][0], f)
    cols = []
    qa, ka, va = w_in[:, 0:512], w_in[:, 512:1024], w_in[:, 1024:1536]
    for h in range(8):
        q = qa[:, h * 64:(h + 1) * 64]
        k = ka[:, h * 64:(h + 1) * 64]
        cols += [q, np.concatenate([q[:, 32:], q[:, :32]], 1), k, np.concatenate([k[:, 32:], k[:, :32]], 1), va[:, h * 64:(h + 1) * 64]]
    cols.append(w_in[:, 1536:1792])
    cols.append(w_in[:, 1792:1920])
    kpe = w_in[:, 1920:1952]
    cols += [kpe, np.concatenate([kpe[:, 16:], kpe[:, :16]], 1)]
    cols.append(w_in[:, 1952:4000])
    w_inr = np.concatenate(cols, 1)
    assert w_inr.shape[1] == NCOL
    w_inr = np.ascontiguousarray(w_inr.reshape(8, 128, NCOL).transpose(1, 0, 2))

    def fm(v):
        return np.ascontiguousarray(np.asarray(v, f).reshape(-1, 128).T)

    wqu = np.asarray(inputs["w_q_up"][0], f).reshape(2, 128, 8, 96)
    nrm = wqu
    swp = np.concatenate([wqu[..., :64], wqu[..., 80:96], wqu[..., 64:80]], -1)
    wqu_r = np.stack([nrm, swp], 0)
    wqu_r = np.ascontiguousarray(wqu_r.transpose(2, 3, 1, 0, 4))
    wkv = np.ascontiguousarray(np.asarray(inputs["w_kv_up"][0], f).reshape(128, 8, 128))

    def rowchunks(w):
        w = np.asarray(w, f)
        return np.ascontiguousarray(w.reshape(-1, 128, w.shape[1]).transpose(1, 0, 2))

    k1 = np.asarray(inputs["peer_sub_keys_1"][0], f)
    k2 = np.asarray(inputs["peer_sub_keys_2"][0], f)
    keys = np.stack([k1, k2], 1).reshape(16, 128, 128)
    keysT = np.ascontiguousarray(keys.transpose(2, 0, 1))
    u = np.asarray(inputs["peer_expert_u"][0], f)
    uT = np.ascontiguousarray(u.reshape(128, 128, 8, 128).transpose(0, 3, 2, 1)).reshape(128, 128, 1024)
    vE = np.ascontiguousarray(np.asarray(inputs["peer_expert_v"][0], f).reshape(128, 128, 1024))
    shared = {
        "w_inr": w_inr,
        "mixg": fm(inputs["mix_norm_g"][0]),
        "ffng": fm(inputs["ffn_norm_g"][0]),
        "fing": np.ascontiguousarray(np.asarray(inputs["final_norm_g"], f).reshape(1, D)),
        "qng": fm(inputs["q_norm_g"][0]),
        "kvng": fm(inputs["kv_norm_g"][0]),
        "wqu": wqu_r,
        "wkv": wkv,
        "wa": rowchunks(inputs["w_branch_a"][0]),
        "wb": rowchunks(inputs["w_branch_b"][0]),
        "wo": rowchunks(inputs["w_out"][0]),
        "wpq": rowchunks(inputs["w_peer_query"][0]),
        "keysT": keysT,
        "uT": uT,
        "vE": vE,
    }
    return shared


def kernel(**inputs):
    n = 8
    shared = _prep_inputs(inputs)
    x = np.asarray(inputs["x"], np.float32)
    pos = np.asarray(inputs["positions"], np.int32)
    in_maps = []
    for c in range(n):
        m = dict(shared)
        m["x"] = np.ascontiguousarray(x[c])
        m["pos"] = np.ascontiguousarray(pos[c].reshape(1, S_LEN))
        in_maps.append(m)
    nc = build_nc()
    res = run_bass_kernel_spmd(nc, in_maps, core_ids=list(range(n)))
    return np.stack([np.asarray(r["out"], np.float32) for r in res.results], 0)
```

```python
import contextlib
import math
import numpy as np
import concourse.bass as bass
import concourse.mybir as mybir
from concourse.alu_op_type import AluOpType as ALU
from concourse.bass_utils import run_bass_kernel_spmd

F32 = mybir.dt.float32
BF16 = mybir.dt.bfloat16
I32 = mybir.dt.int32
U32 = mybir.dt.uint32
AF = mybir.ActivationFunctionType
AX = mybir.AxisListType

S_LEN = 4096
D = 1024
NT = S_LEN // 128
NG = S_LEN // 512
NCOL = 8 * 320 + 256 + 128 + 64 + 2048
OFF_CQ = 2560
OFF_CKV = 2816
OFF_KPE = 2944
OFF_GA = 3008
OFF_GB = 4032
EPS = 1e-6
THETA = 10000.0
BIG = 8192.0
PG = 256
NPG = S_LEN // PG


class Tok:
    __slots__ = ("name", "last_w", "readers")

    def __init__(self, name=""):
        self.name = name
        self.last_w = None
        self.readers = {}


class Sched:
    ENG = ("pe", "act", "dve", "pool", "sp")
    NDMA = 48
    NDMA_SW = 24

    def __init__(self, nc, stack):
        self.nc = nc
        self.q = {e: [] for e in self.ENG}
        self.cnt = {e: 0 for e in self.ENG}
        self.seen = {e: {} for e in self.ENG}
        self.sem = {e: stack.enter_context(nc.semaphore("c_" + e)) for e in self.ENG}
        self.dsem = {"sw": [stack.enter_context(nc.semaphore("ds%d" % i)) for i in range(self.NDMA_SW)],
                     "hw": [stack.enter_context(nc.semaphore("dh%d" % i)) for i in range(self.NDMA)]}
        self.ndma = {"sw": 0, "hw": 0}
        self.last_dma = {}
        self.same_engine_sync = True
        self.nosync_engines = ("pe",)
        self.stack = stack

    def sb(self, name, shape, dt):
        return self.stack.enter_context(self.nc.sbuf_tensor(name, shape, dt))

    def ps(self, name, shape, dt=F32):
        return self.stack.enter_context(self.nc.psum_tensor(name, shape, dt))

    def _deps(self, eng, reads, writes):
        deps = []
        for t in reads:
            if t.last_w is not None:
                deps.append(t.last_w)
        for t in writes:
            if t.last_w is not None:
                deps.append(t.last_w)
            deps.extend(t.readers.values())
        waits = {}
        own = self.sem[eng]
        for (s, v) in deps:
            if s is own and (eng in self.nosync_engines or not self.same_engine_sync):
                continue
            k = id(s)
            if self.seen[eng].get(k, 0) >= v:
                continue
            if k not in waits or waits[k][1] < v:
                waits[k] = (s, v)
        for k, (s, v) in waits.items():
            self.seen[eng][k] = v
        return list(waits.values())

    def _mark(self, tok, reads, writes):
        for t in reads:
            k = id(tok[0])
            if k not in t.readers or t.readers[k][1] < tok[1]:
                t.readers[k] = tok
        for t in writes:
            t.last_w = tok
            t.readers = {}

    def op(self, eng, fn, reads=(), writes=()):
        waits = self._deps(eng, reads, writes)
        self.cnt[eng] += 1
        tok = (self.sem[eng], self.cnt[eng])
        self.q[eng].append((waits, fn, (self.sem[eng], 1)))
        self._mark(tok, reads, writes)
        return tok

    def dma(self, eng, fn, reads=(), writes=()):
        waits = self._deps(eng, reads, writes)
        kind = "sw" if eng == "pool" else "hw"
        pool_ = self.dsem[kind]
        i = self.ndma[kind]
        self.ndma[kind] += 1
        s = pool_[i % len(pool_)]
        k = i // len(pool_)
        if k > 0:
            need = 16 * k
            if self.seen[eng].get(id(s), 0) < need:
                waits.append((s, need))
                self.seen[eng][id(s)] = need
        tok = (s, 16 * (k + 1))
        self.q[eng].append((waits, fn, (s, 16)))
        self._mark(tok, reads, writes)
        self.last_dma[id(s)] = tok
        return tok

    def barrier(self):
        toks = [(self.sem[e], self.cnt[e]) for e in self.ENG if self.cnt[e] > 0]
        toks += list(self.last_dma.values())
        for e in self.ENG:
            waits = []
            for (s, v) in toks:
                if s is self.sem[e]:
                    continue
                if self.seen[e].get(id(s), 0) < v:
                    waits.append((s, v))
                    self.seen[e][id(s)] = v
            self.q[e].append((waits, None, None))

    def wait_all(self, eng, toks):
        waits = []
        for (s, v) in toks:
            if self.seen[eng].get(id(s), 0) < v:
                waits.append((s, v))
                self.seen[eng][id(s)] = v
        self.q[eng].append((waits, None, None))

    def emit(self):
        nc = self.nc
        hmap = {"pe": "tensor", "act": "scalar", "dve": "vector", "pool": "gpsimd", "sp": "sync"}
        with nc.Block() as block:
            for ename in self.ENG:
                items = self.q[ename]

                def body(e, items=items):
                    for (waits, fn, inc) in items:
                        for (s, v) in waits:
                            e.wait_ge(s, v)
                        if fn is not None:
                            ins = fn(e)
                            if inc is not None:
                                ins.then_inc(inc[0], inc[1])

                getattr(block, hmap[ename])(body)


def build_nc(dbg=False, stop_after=None, skip=()):
    nc = bass.Bass("TRN2", target_bir_lowering=False)

    def din(name, shape, dt=F32):
        return nc.dram_tensor(name, shape, dt, kind="ExternalInput").ap()

    x_d = din("x", [S_LEN, D])
    pos_d = din("pos", [1, S_LEN], I32)
    w_in_d = din("w_inr", [128, 8, NCOL])
    mixg_d = din("mixg", [128, 8])
    ffng_d = din("ffng", [128, 8])
    fing_d = din("fing", [1, D])
    qng_d = din("qng", [128, 2])
    kvng_d = din("kvng", [128, 1])
    wqu_d = din("wqu", [128, 8, 2, 2, 96])
    wkv_d = din("wkv", [128, 8, 128])
    wa_d = din("wa", [128, 4, D])
    wb_d = din("wb", [128, 4, D])
    wo_d = din("wo", [128, 8, D])
    wpq_d = din("wpq", [128, 8, 2048])
    keys_d = din("keysT", [128, 16, 128])
    ut_d = din("uT", [128, 128, 1024])
    v_d = din("vE", [128, 128, 1024])
    out_d = nc.dram_tensor("out", [S_LEN, D], F32, kind="ExternalOutput").ap()
    kind_scr = "ExternalOutput" if dbg else "Internal"
    x1_d = nc.dram_tensor("x1s", [S_LEN, D], F32, kind=kind_scr).ap()
    us_d = nc.dram_tensor("us", [128, 128, 1024], BF16, kind="Internal").ap()
    vs_d = nc.dram_tensor("vs", [128, 128, 1024], BF16, kind="Internal").ap()
    wpqs_d = nc.dram_tensor("wpqs", [128, 8, 2048], BF16, kind="Internal").ap()
    if dbg:
        ya_d = nc.dram_tensor("dbg_ya", [128, 4, S_LEN], BF16, kind="ExternalOutput").ap()
        yb_d = nc.dram_tensor("dbg_yb", [128, 4, S_LEN], BF16, kind="ExternalOutput").ap()
        pe_d = nc.dram_tensor("dbg_pe", [S_LEN, D], F32, kind="ExternalOutput").ap()
        misc_d = nc.dram_tensor("dbg_misc", [128, 9, 4096], BF16, kind="ExternalOutput").ap()
        x1a_d = nc.dram_tensor("dbg_x1a", [S_LEN, D], F32, kind="ExternalOutput").ap()

    with contextlib.ExitStack() as st:
        S = Sched(nc, st)
        ARENA_N = 32768 + 60416
        arena = S.sb("arena", [128, ARENA_N], BF16)

        def carve(off, n):
            return arena[:, off:off + n]

        hT = carve(0, 32768).rearrange("p (c t) -> p c t", c=8)
        O = 32768
        YT4 = carve(O, 16384).rearrange("p (c t) -> p c t", c=4)
        O2 = O + 16384
        ropeC = carve(O2, 4096)
        ropeS = carve(O2 + 4096, 4096)
        qaug = carve(O2 + 8192, 4096)
        kaug = carve(O2 + 12288, 4096)
        vaug = carve(O2 + 16384, 2112)[:, 0:NT * 65].rearrange("p (t f) -> p t f", t=NT)
        wh = [carve(O2 + 18496, 3584), carve(O2 + 18496 + 3584, 2560)]
        PTb = [carve(O2 + 18496 + 6144 + i * 512, 512) for i in range(3)]
        o3 = O2 + 18496 + 6144 + 1536
        cqT = carve(o3, 8192).rearrange("p (c t) -> p c t", c=2)
        ckvT = carve(o3 + 8192, 4096)
        o4 = o3 + 12288
        wqu_sb = [carve(o4 + i * 384, 384).rearrange("p (c s f) -> p c s f", c=2, s=2) for i in range(2)]
        wkv_sb = [carve(o4 + 768 + i * 128, 128) for i in range(2)]
        att_end = o4 + 1024
        assert att_end <= ARENA_N, att_end
        YT = carve(O2, 2048).rearrange("p (c t) -> p c t", c=4)
        ma = carve(O2 + 2048, 4096).rearrange("p (c t) -> p c t", c=8)
        wab = carve(O2 + 6144, 4096).rearrange("p (c f) -> p c f", c=4)
        wo_sb = carve(O2 + 10240, 8192).rearrange("p (c f) -> p c f", c=8)
        wg = [carve(O2 + 18432 + i * 1024, 1024).rearrange("p (c f) -> p c f", c=8) for i in range(2)]
        sgb = [carve(O2 + 20480 + i * 512, 512) for i in range(2)]
        Wbuf = carve(O, PG * 128).rearrange("p (t i) -> p t i", t=PG)
        p1 = O + PG * 128
        NUB = 4
        ub = [carve(p1 + i * 1024, 1024).rearrange("p (c j) -> p c j", c=8) for i in range(NUB)]
        vb = [carve(p1 + NUB * 1024 + i * 1024, 1024) for i in range(NUB)]
        p2 = p1 + 2 * NUB * 1024
        qTs = carve(p2, 16 * PG).rearrange("p (q t) -> p q t", q=16)
        p3 = p2 + 16 * PG
        keys_sb = carve(p3, 2048).rearrange("p (q n) -> p q n", q=16)
        p4 = p3 + 2048
        wpq_sb = carve(p4, 2048).rearrange("p (c f) -> p c f", c=8)
        p5 = p4 + 2048
        NAB = 4
        At = [carve(p5 + i * 128, 128) for i in range(NAB)]
        Bt = [carve(p5 + NAB * 128 + i * 128, 128) for i in range(NAB)]
        p6 = p5 + 2 * NAB * 128
        geb = [carve(p6 + i * PG, PG) for i in range(2)]
        wab2 = [carve(p6 + 2 * PG + i * PG, PG) for i in range(2)]
        p7 = p6 + 4 * PG

        def carve32(off, n, dt=F32):
            return carve(off, 2 * n).bitcast(dt)

        s_all = carve32(p7, 2048).rearrange("p (q n) -> p q n", q=16)
        eq = carve32(p7, 2048).rearrange("p (h k a) -> p h k a", h=8, k=16)
        xwP = carve32(p7, 1024)
        p7 += 4096
        cand1 = carve32(p7, 256)
        best = carve32(p7 + 512, 128).rearrange("p (a b) -> p a b", a=8)
        posu = carve32(p7 + 768, 128, U32).rearrange("p (a b) -> p a b", a=8)
        abu = carve32(p7 + 1024, 256, U32).rearrange("p (s a b) -> p s a b", s=2, a=8)
        abf = carve32(p7 + 1536, 256).rearrange("p (s a b) -> p s a b", s=2, a=8)
        GIJ = carve32(p7 + 2048, 384).rearrange("p (a b) -> p a b", a=3)
        GIJT = carve32(p7 + 2816, 384).rearrange("p (a b) -> p a b", a=3)
        ez = carve32(p7 + 3584, 128).rearrange("p (a b) -> p a b", a=8)
        zs = carve32(p7 + 3840, 16)
        GIJT2 = [GIJT, carve32(p7 + 3872, 384).rearrange("p (a b) -> p a b", a=3)]
        peer_end = p7 + 3872 + 768
        assert peer_end <= ARENA_N, peer_end

        identb = S.sb("identb", [128, 128], BF16)
        identf = S.sb("identf", [128, 128], F32)
        tri = S.sb("tri", [128, 128], BF16)
        onesb = S.sb("onesb", [128, 128], BF16)
        iof = S.sb("iof", [128, 128], F32)
        iob = S.sb("iob", [128, 128], BF16)
        onesf = S.sb("onesf", [128, 64], F32)
        ioi = S.sb("ioi", [128, 128], I32)
        xt = [S.sb("xt%d" % i, [128, D], F32) for i in range(2)]
        wkall = S.sb("wkall", [128, 4, 512], F32)
        wk = [wkall[:, i, :] for i in range(4)]
        xw = wkall[:, 0:2, :].rearrange("p a b -> p (a b)")
        fing = xw
        wk23 = wkall[:, 2:4, :].rearrange("p a b -> p (a b)")
        swk = wk23[:, 0:256]
        v12 = wk23[:, 256:512].rearrange("p (h s k) -> p h s k", h=8, s=2)
        i12u = wk23[:, 512:768].bitcast(U32).rearrange("p (h s k) -> p h s k", h=8, s=2)
        i12f = wk23[:, 768:1024].rearrange("p (h s k) -> p h s k", h=8, s=2)
        wki = S.sb("wki", [128, 512], I32)
        mixg = S.sb("mixg_sb", [128, 8], F32)
        ffng = S.sb("ffng_sb", [128, 8], F32)
        qng = S.sb("qng_sb", [128, 2], F32)
        kvng = S.sb("kvng_sb", [128, 1], F32)
        sm = S.sb("sm", [128, 64], F32)
        ksum = S.sb("ksum", [128, 16], F32)
        kmT = S.sb("kmT", [128, 16], BF16)
        padrow = S.sb("padrow", [128, 16, 16], F32)
        gm = S.sb("gm", [128, 16], F32)
        top8 = S.sb("top8", [128, 8], F32)
        biasq = S.sb("biasq", [128, 16], BF16)
        rec = S.sb("rec", [128, 8], F32)
        GIJb = S.sb("GIJb", [128, 3, 128], F32)

        pb = [S.ps("pb%d" % i, [128, 512], F32) for i in range(8)]
        pbT = [Tok("pb%d" % i) for i in range(8)]
        pb7h = pb[7][:].bitcast(BF16)

        T = {}

        def tk(name):
            if name not in T:
                T[name] = Tok(name)
            return T[name]

        hTt = [tk("hT%d" % g) for g in range(NG)]
        Yt = [tk("Y%d" % g) for g in range(NG)]
        qt_ = [tk("qaug%d" % g) for g in range(NG)]
        kt_ = [tk("kaug%d" % g) for g in range(NG)]
        vt_ = [tk("vaug%d" % g) for g in range(NG)]
        xtT = [tk("xt0"), tk("xt1")]
        wkT = [tk("wk%d" % i) for i in range(4)]

        def MM(out, lhsT, rhs, start, stop, reads, writes):
            S.op("pe", lambda e: e.matmul(out, lhsT=lhsT, rhs=rhs, start=start, stop=stop), reads, writes)

        def TR(out, in_, ident, reads, writes):
            S.op("pe", lambda e: e.transpose(out=out, in_=in_, identity=ident), reads, writes)

        def ACT(out, in_, func, reads, writes, scale=None, bias=None):
            kw = {}
            if scale is not None:
                kw["scale"] = scale
            if bias is not None:
                kw["bias"] = bias
            S.op("act", lambda e: e.activation(out=out, in_=in_, func=func, **kw), reads, writes)

        def TT(eng, out, in0, in1, op, reads, writes):
            S.op(eng, lambda e: e.tensor_tensor(out=out, in0=in0, in1=in1, op=op), reads, writes)

        def TS(eng, out, in0, s1, s2, op0, op1, reads, writes):
            if op1 is None:
                S.op(eng, lambda e: e.tensor_scalar(out=out, in0=in0, scalar1=s1, scalar2=None, op0=op0), reads, writes)
            else:
                S.op(eng, lambda e: e.tensor_scalar(out=out, in0=in0, scalar1=s1, scalar2=s2, op0=op0, op1=op1), reads, writes)

        def CP(eng, out, in_, reads, writes):
            if eng == "act":
                S.op("act", lambda e: e.copy(out=out, in_=in_), reads, writes)
            else:
                S.op(eng, lambda e: e.tensor_copy(out=out, in_=in_), reads, writes)

        def MS(eng, ap, val, writes):
            S.op(eng, lambda e: e.memset(ap, val), (), writes)

        def DMA(eng, out, in_, reads, writes):
            return S.dma(eng, lambda e: e.dma_start(out=out, in_=in_), reads, writes)

        bank_rr = [0]

        def next_bank(n=6):
            b = bank_rr[0] % n
            bank_rr[0] += 1
            return b

        stashT = [tk("stash_u"), tk("stash_v")]

        def stash_tables(lo, hi):
            for i in range(lo, hi):
                DMA("pool", us_d[i], ut_d[i], (), [stashT[0]])
                DMA("pool", vs_d[i], v_d[i], (), [stashT[1]])

        S.op("pool", lambda e: e.iota(ioi[:], pattern=[[1, 128]], base=0, channel_multiplier=0), (), [tk("ioi")])
        CP("dve", iof[:], ioi[:], [tk("ioi")], [tk("iof")])
        CP("dve", iob[:], ioi[:], [tk("ioi")], [tk("iob")])
        MS("dve", onesf[:], 1.0, [tk("onesf")])
        S.op("pool", lambda e: e.iota(wki[:, 0:128], pattern=[[1, 128]], base=0, channel_multiplier=-1), (), [tk("wki")])
        CP("dve", wk[0][:, 0:128], wki[:, 0:128], [tk("wki")], [wkT[0]])
        S.op("dve", lambda e: e.tensor_single_scalar(out=identb[:], in_=wk[0][:, 0:128], scalar=0.0, op=ALU.is_equal), [wkT[0]], [tk("identb")])
        S.op("dve", lambda e: e.tensor_single_scalar(out=identf[:], in_=wk[0][:, 0:128], scalar=0.0, op=ALU.is_equal), [wkT[0]], [tk("identf")])
        S.op("dve", lambda e: e.tensor_single_scalar(out=tri[:], in_=wk[0][:, 0:128], scalar=0.0, op=ALU.is_ge), [wkT[0]], [tk("tri")])
        MS("dve", onesb[:], 1.0, [tk("onesb")])
        S.op("pool", lambda e: e.iota(wki[:, 0:256], pattern=[[-1, 16], [1, 16]], base=0, channel_multiplier=0), [wkT[0]], [tk("wki")])
        CP("dve", wk[1][:, 0:256], wki[:, 0:256], [tk("wki")], [wkT[1]])
        TS("dve", padrow[:].rearrange("p a b -> p (a b)"), wk[1][:, 0:256], 0.0, -1e30, ALU.is_ge, ALU.mult, [wkT[1]], [tk("padrow")])
        DMA("sp", mixg[:], mixg_d[:, :], (), [tk("mixg")])
        DMA("sp", ffng[:], ffng_d[:, :], (), [tk("ffng")])
        DMA("sp", qng[:], qng_d[:, :], (), [tk("qng")])
        DMA("sp", kvng[:], kvng_d[:, :], (), [tk("kvng")])

        def build_rope(p0, n, half, ident_rows):
            tR = tk("rope")
            smT = tk("sm")
            pr = slice(p0, p0 + n)
            if ident_rows:
                MS("pool", ropeC[0:ident_rows, :], 1.0, [tR])
                MS("pool", ropeS[0:ident_rows, :], 0.0, [tR])
            S.op("pool", lambda e: e.iota(wki[pr, 0:1], pattern=[[0, 1]], base=0, channel_multiplier=1), [wkT[1]], [tk("wki")])
            CP("dve", sm[pr, 0:1], wki[pr, 0:1], [tk("wki")], [smT])
            TS("dve", sm[pr, 1:2], sm[pr, 0:1], float(half), None, ALU.is_ge, None, [smT], [smT])
            S.op("dve", lambda e: e.scalar_tensor_tensor(out=sm[pr, 2:3], in0=sm[pr, 1:2], scalar=-float(half), in1=sm[pr, 0:1], op0=ALU.mult, op1=ALU.add), [smT], [smT])
            ACT(sm[pr, 3:4], sm[pr, 2:3], AF.Exp, [smT], [smT], scale=-math.log(THETA) / half)
            TS("dve", sm[pr, 4:5], sm[pr, 1:2], 2.0, -1.0, ALU.mult, ALU.add, [smT], [smT])
            for cch in range(S_LEN // 512):
                cs = slice(cch * 512, (cch + 1) * 512)
                DMA("sp", wki[pr, :], pos_d[0:1, cs].to_broadcast([n, 512]), [wkT[0]], [tk("wki")])
                CP("dve", wk[0][pr, :], wki[pr, :], [tk("wki")], [wkT[0]])
                TS("dve", wk[0][pr, :], wk[0][pr, :], sm[pr, 3:4], None, ALU.mult, None, [wkT[0], smT], [wkT[0]])
                for which in range(2):
                    a = wk[1][pr, :]
                    if which == 0:
                        CP("dve", a, wk[0][pr, :], [wkT[0]], [wkT[1]])
                    else:
                        TS("dve", a, wk[0][pr, :], math.pi / 2, None, ALU.add, None, [wkT[0]], [wkT[1]])
                    TS("dve", wk[2][pr, :], a, 1.0 / (2 * math.pi), None, ALU.mult, None, [wkT[1]], [wkT[2]])
                    CP("dve", wki[pr, :], wk[2][pr, :], [wkT[2]], [tk("wki")])
                    CP("dve", wk[2][pr, :], wki[pr, :], [tk("wki")], [wkT[2]])
                    S.op("dve", lambda e, a=a: e.scalar_tensor_tensor(out=a, in0=wk[2][pr, :], scalar=-2 * math.pi, in1=a, op0=ALU.mult, op1=ALU.add), [wkT[1], wkT[2]], [wkT[1]])
                    TS("dve", wk[2][pr, :], a, math.pi, -2 * math.pi, ALU.is_gt, ALU.mult, [wkT[1]], [wkT[2]])
                    TT("dve", a, a, wk[2][pr, :], ALU.add, [wkT[1], wkT[2]], [wkT[1]])
                    TS("dve", wk[2][pr, :], a, -math.pi, 2 * math.pi, ALU.is_lt, ALU.mult, [wkT[1]], [wkT[2]])
                    TT("dve", a, a, wk[2][pr, :], ALU.add, [wkT[1], wkT[2]], [wkT[1]])
                    TS("dve", a, a, -math.pi, math.pi, ALU.max, ALU.min, [wkT[1]], [wkT[1]])
                    if which == 0:
                        ACT(wk[3][pr, :], a, AF.Sin, [wkT[1]], [wkT[3]])
                        TS("dve", ropeS[pr, cs], wk[3][pr, :], sm[pr, 4:5], None, ALU.mult, None, [wkT[3], smT], [tR])
                    else:
                        ACT(ropeC[pr, cs], a, AF.Sin, [wkT[1]], [tR])

        def norm_to_hT(src, srcT, tile_idx, gains, gT):
            g = tile_idx // 4
            xw_ = xw if tile_idx % 2 == 0 else wk23
            xwT = tk("xw%d" % (tile_idx % 2))
            sc = 8 + 4 * (tile_idx % 2)
            ACT(xw_[:], src, AF.Square, [srcT], [xwT])
            S.op("dve", lambda e, xw_=xw_, sc=sc: e.tensor_reduce(out=sm[:, sc:sc + 1], in_=xw_[:], axis=AX.X, op=ALU.add), [xwT], [tk("smA%d" % sc)])
            ACT(sm[:, sc + 1:sc + 2], sm[:, sc:sc + 1], AF.Sqrt, [tk("smA%d" % sc)], [tk("smB%d" % sc)], scale=1.0 / D, bias=EPS)
            S.op("dve", lambda e, sc=sc: e.reciprocal(out=sm[:, sc + 2:sc + 3], in_=sm[:, sc + 1:sc + 2]), [tk("smB%d" % sc)], [tk("smC%d" % sc)])
            TS("dve", xw_[:], src, sm[:, sc + 2:sc + 3], None, ALU.mult, None, [srcT, tk("smC%d" % sc)], [xwT])
            for half in range(2):
                b = next_bank()
                for c4 in range(4):
                    c = half * 4 + c4
                    TR(pb[b][:, c4 * 128:(c4 + 1) * 128], xw_[:, c * 128:(c + 1) * 128], identf[:], [xwT, tk("identf")], [pbT[b]])
                for c4 in range(4):
                    c = half * 4 + c4
                    ACT(hT[:, c, tile_idx * 128:(tile_idx + 1) * 128], pb[b][:, c4 * 128:(c4 + 1) * 128], AF.Copy,
                        [pbT[b], gT], [hTt[g]], scale=gains[:, c:c + 1])

        whT = [tk("wh0"), tk("wh1")]

        def load_wh_moba(h):
            DMA("pool", wh[h % 2][:, 0:2560].rearrange("p (c f) -> p c f", c=8), w_in_d[:, :, h * 320:(h + 1) * 320], (), [whT[h % 2]])

        S.barrier()
        load_wh_moba(0)
        DMA("pool", wpqs_d[:, :, :], wpq_d[:, :, :], (), [tk("wpqs")])
        stash_tables(0, 16)

        DMA("sp", xt[0][:], x_d[0:128, :], (), [xtT[0]])
        for t in range(NT):
            if t + 1 < NT:
                DMA("sp", xt[(t + 1) % 2][:], x_d[(t + 1) * 128:(t + 2) * 128, :], (), [xtT[(t + 1) % 2]])
            norm_to_hT(xt[t % 2][:], xtT[t % 2], t, mixg, tk("mixg"))

        S.barrier()

        def attention(h, K, scale):
            hp = (h % 2) * 64
            pairs = [(Q, j) for Q in range(NG) for j in range(4 * Q + 4)]
            sbanks = [0, 1, 5, 6]
            SKEW = 3

            def emit_S(k):
                Q, j = pairs[k]
                r = j - 4 * Q
                q0 = 128 * max(r, 0)
                sb_ = sbanks[k % 4]
                MM(pb[sb_][:, q0:512], kaug[0:K, j * 128:(j + 1) * 128], qaug[0:K, Q * 512 + q0:(Q + 1) * 512], True, True,
                   [kt_[j // 4], qt_[Q]], [pbT[sb_]])

            def emit_norm_pe(Q):
                MM(pb[4][0:64, :], onesf[64:65, :], wk[3][64:65, :], True, True, [tk("onesf"), wkT[3]], [pbT[4]])
                S.op("dve", lambda e: e.reciprocal(out=wk[2][0:64, :], in_=pb[4][0:64, :]), [pbT[4]], [wkT[2]])
                ab = 2 + (Q % 2)
                TT("dve", YT4[hp:hp + 64, h // 2, Q * 512:(Q + 1) * 512], pb[ab][0:64, :], wk[2][0:64, :], ALU.mult, [pbT[ab], wkT[2]], [Yt[Q]])

            for k in range(min(SKEW, len(pairs))):
                emit_S(k)
            deferred = {}
            for k in range(len(pairs)):
                Q, j = pairs[k]
                nj = 4 * Q + 4
                ab = 2 + (Q % 2)
                r = j - 4 * Q
                q0 = 128 * max(r, 0)
                sb_ = sbanks[k % 4]
                pt = PTb[k % 3]
                ptT = tk("PT%d" % (k % 3))
                if k + SKEW < len(pairs):
                    emit_S(k + SKEW)
                ACT(pt[:, q0:512], pb[sb_][:, q0:512], AF.Exp, [pbT[sb_]], [ptT], scale=scale)
                if r >= 0:
                    TT("dve", pt[:, q0:q0 + 128], pt[:, q0:q0 + 128], tri[:], ALU.mult, [ptT, tk("tri")], [ptT])
                MM(pb[ab][0:65, q0:512], vaug[:, j, :], pt[:, q0:512], j == 0, j == nj - 1, [ptT, vt_[j // 4]], [pbT[ab]])
                if k in deferred:
                    emit_norm_pe(deferred.pop(k))
                if j == nj - 1:
                    CP("act", wk[3][64:65, :], pb[ab][64:65, :], [pbT[ab]], [wkT[3]])
                    if k + 2 < len(pairs):
                        deferred[k + 2] = Q
                    else:
                        emit_norm_pe(Q)

        def rope_combine(rows, cs_tok, pq, pqs, out_ap, outT, want_f32=None, s_off=0):
            TT("dve", wk[0][0:rows, :], pb[pqs][s_off:s_off + rows, :], ropeS[s_off:s_off + rows, cs_tok], ALU.mult, [pbT[pqs], tk("rope")], [wkT[0]])
            TT("dve", wk[1][0:rows, :], pb[pq][0:rows, :], ropeC[0:rows, cs_tok], ALU.mult, [pbT[pq], tk("rope")], [wkT[1]])
            if want_f32 is None:
                TT("pool", out_ap, wk[0][0:rows, :], wk[1][0:rows, :], ALU.add, [wkT[0], wkT[1]], [outT])
            else:
                TT("pool", want_f32, wk[0][0:rows, :], wk[1][0:rows, :], ALU.add, [wkT[0], wkT[1]], [wkT[2]])
                CP("act", out_ap, want_f32, [wkT[2]], [outT])

        build_rope(0, 64, 32, 0)
        CP("act", ropeS[64:128, :], ropeS[0:64, :], [tk("rope")], [tk("rope")])
        S.op("pool", lambda e: e.iota(wki[64:80, :], pattern=[[1, 2], [0, 256]], base=0, channel_multiplier=-1), [wkT[0]], [tk("wki")])
        for cch in range(8):
            CP("dve", wk[0][64:80, :], wki[64:80, :], [tk("wki")], [wkT[0]])
            TS("dve", kaug[64:80, cch * 512:(cch + 1) * 512], wk[0][64:80, :], float(-2 * cch), None, ALU.is_equal, None, [wkT[0]], [kt_[cch]])
        MS("pool", qaug[64:80, :], 0.0, [qt_[g] for g in range(NG)])
        MS("pool", vaug[:, :, 64:65], 1.0, [vt_[g] for g in range(NG)])

        for h in range(8):
            if "moba" in skip:
                stash_tables(16 + h * 14, 16 + (h + 1) * 14)
                continue
            w3 = wh[h % 2][:, 0:2560].rearrange("p (c f) -> p c f", c=8)
            wT_ = whT[h % 2]
            if h + 1 < 8:
                load_wh_moba(h + 1)
            stash_tables(16 + h * 14, 16 + (h + 1) * 14)
            for g in range(NG):
                cs = slice(g * 512, (g + 1) * 512)
                bq, bk, bv = (0, 1, 2) if g % 2 == 0 else (3, 4, 5)
                for c in range(8):
                    MM(pb[bq][:, :], w3[:, c, 0:128], hT[:, c, cs], c == 0, c == 7, [wT_, hTt[g]], [pbT[bq]])
                for c in range(8):
                    MM(pb[bk][:, :], w3[:, c, 128:256], hT[:, c, cs], c == 0, c == 7, [wT_, hTt[g]], [pbT[bk]])
                for tt in range(4):
                    for c in range(8):
                        MM(pb[bv][:, tt * 64:(tt + 1) * 64], hT[:, c, g * 512 + tt * 128:g * 512 + (tt + 1) * 128], w3[:, c, 256:320],
                           c == 0, c == 7, [wT_, hTt[g]], [pbT[bv]])
                rope_combine(64, cs, bq, bq, qaug[0:64, cs], qt_[g], s_off=64)
                rope_combine(64, cs, bk, bk, kaug[0:64, cs], kt_[g], want_f32=wk[2][0:64, :], s_off=64)
                S.op("dve", lambda e, g=g: e.tensor_reduce(out=ksum[0:64, 2 * g:2 * g + 2], in_=wk[2][0:64, :].rearrange("p (b k) -> p b k", b=2),
                                                      axis=AX.X, op=ALU.add), [wkT[2]], [tk("ksum")])
                CP("act", vaug[:, g * 4:(g + 1) * 4, 0:64], pb[bv][:, 0:256].rearrange("p (t f) -> p t f", t=4), [pbT[bv]], [vt_[g]])
            S.op("act", lambda e: e.mul(out=kmT[0:64, :], in_=ksum[0:64, :], mul=1.0 / 256), [tk("ksum")], [tk("kmT")])
            for qt in range(8, NT):
                blk = qt // 2
                g = qt // 4
                MM(pb[6][:, 0:16], qaug[0:64, qt * 128:(qt + 1) * 128], kmT[0:64, :], True, True, [qt_[g], tk("kmT")], [pbT[6]])
                TT("dve", gm[:], pb[6][:, 0:16], padrow[:, blk, :], ALU.add, [pbT[6], tk("padrow")], [tk("gm")])
                S.op("dve", lambda e: e.max(out=top8[:], in_=gm[:]), [tk("gm")], [tk("top8")])
                MS("pool", biasq[:], 0.0, [tk("biasq")])
                TS("dve", biasq[:, 0:blk], gm[:, 0:blk], top8[:, 2:3], -BIG, ALU.is_lt, ALU.mult, [tk("gm"), tk("top8")], [tk("biasq")])
                TR(pb7h[64:80, 0:128], biasq[:], identb[:], [tk("biasq"), tk("identb")], [pbT[7]])
                CP("act", qaug[64:80, qt * 128:(qt + 1) * 128], pb7h[64:80, 0:128], [pbT[7]], [qt_[g]])
            attention(h, 80, 1.0 / 8.0)

        if dbg:
            DMA("sp", ya_d[:, :, :], YT4[:, :, :], Yt, [tk("dbg_ya")])

        def branch(which):
            wsrc = wa_d if which == 0 else wb_d
            goff = OFF_GA if which == 0 else OFF_GB
            S.barrier()
            DMA("pool", wab[:, :, :], wsrc[:, :, :], (), [tk("wab")])
            DMA("pool", wo_sb[:, :, :], wo_d[:, :, :], (), [tk("wo")])
            wgT = [tk("wg0"), tk("wg1")]
            sgT = [tk("sg0"), tk("sg1")]
            base_d = x_d if which == 0 else x1_d
            DMA("pool", wg[0][:, :, :], w_in_d[:, :, goff:goff + 128], (), [wgT[0]])
            for g in range(NG):
                cs = slice(g * 512, (g + 1) * 512)
                for oc in range(8):
                    k_ = g * 8 + oc
                    if k_ + 1 < NG * 8:
                        oc2 = (oc + 1) % 8
                        DMA("pool", wg[(k_ + 1) % 2][:, :, :], w_in_d[:, :, goff + oc2 * 128:goff + (oc2 + 1) * 128], (), [wgT[(k_ + 1) % 2]])
                    bb = next_bank(2)
                    bg = 2 + next_bank(2) % 2
                    for kc in range(4):
                        MM(pb[bb][:, :], wab[:, kc, oc * 128:(oc + 1) * 128], YT4[:, kc, cs], kc == 0, kc == 3, [tk("wab"), Yt[g]], [pbT[bb]])
                    for c in range(8):
                        MM(pb[bg][:, :], wg[k_ % 2][:, c, :], hT[:, c, cs], c == 0, c == 7, [wgT[k_ % 2], hTt[g]], [pbT[bg]])
                    ACT(sgb[k_ % 2][:, :], pb[bg][:, :], AF.Sigmoid, [pbT[bg]], [sgT[k_ % 2]])
                    TT("dve", ma[:, oc, :], pb[bb][:, :], sgb[k_ % 2][:, :], ALU.mult, [pbT[bb], sgT[k_ % 2]], [tk("ma")])
                for tt in range(4):
                    ti = g * 4 + tt
                    xb = xt[ti % 2]
                    xbT = xtT[ti % 2]
                    DMA("sp", xb[:], base_d[ti * 128:(ti + 1) * 128, :], [] if which == 0 else [tk("x1s%d" % ti)], [xbT])
                    for dh in range(2):
                        bo = 4 + dh
                        for oc in range(8):
                            MM(pb[bo][:, :], ma[:, oc, tt * 128:(tt + 1) * 128], wo_sb[:, oc, dh * 512:(dh + 1) * 512], oc == 0, oc == 7,
                               [tk("ma"), tk("wo")], [pbT[bo]])
                        TT("dve", xb[:, dh * 512:(dh + 1) * 512], pb[bo][:, :], xb[:, dh * 512:(dh + 1) * 512], ALU.add, [pbT[bo], xbT], [xbT])
                    DMA("sp", x1_d[ti * 128:(ti + 1) * 128, :], xb[:], [xbT], [tk("x1s%d" % ti)])
                    if which == 1:
                        norm_to_hT(xb[:], xbT, ti, ffng, tk("ffng"))

        branch(0)
        if dbg:
            S.barrier()
            DMA("sp", x1a_d[:, :], x1_d[:, :], [tk("x1s%d" % i) for i in range(NT)], [tk("dbg_x1a")])
        if stop_after == "branch_a":
            pass

        def mla():
            S.barrier()
            build_rope(64, 32, 16, 64)
            MS("pool", vaug[:, :, 64:65], 1.0, [vt_[g] for g in range(NG)])
            wm = wh[0][:, 0:3584].rearrange("p (c f) -> p c f", c=8)
            DMA("pool", wm, w_in_d[:, :, OFF_CQ:OFF_CQ + 448], (), [whT[0]])
            wquT = [tk("wqu0"), tk("wqu1")]
            wkvT = [tk("wkv0"), tk("wkv1")]

            def load_head(h):
                DMA("pool", wqu_sb[h % 2][:, :, :, :], wqu_d[:, h, :, :, :], (), [wquT[h % 2]])
                DMA("pool", wkv_sb[h % 2][:, :], wkv_d[:, h, :], (), [wkvT[h % 2]])

            load_head(0)
            for g in range(NG):
                cs = slice(g * 512, (g + 1) * 512)
                specs = [(0, 0, 128), (1, 128, 256), (2, 256, 384)]
                for (ci, lo, hi) in specs:
                    b = ci
                    for c in range(8):
                        MM(pb[b][:, :], wm[:, c, lo:hi], hT[:, c, cs], c == 0, c == 7, [whT[0], hTt[g]], [pbT[b]])
                sq = [PTb[0], PTb[1], PTb[2]]
                for ci in range(3):
                    ACT(sq[ci][:, :], pb[ci][:, :], AF.Square, [pbT[ci]], [tk("PT%d" % ci)])
                MM(pb[3][:, :], onesb[:], sq[0][:, :], True, False, [tk("onesb"), tk("PT0")], [pbT[3]])
                MM(pb[3][:, :], onesb[:], sq[1][:, :], False, True, [tk("onesb"), tk("PT1")], [pbT[3]])
                MM(pb[4][:, :], onesb[:], sq[2][:, :], True, True, [tk("onesb"), tk("PT2")], [pbT[4]])
                ACT(wk[0][:, :], pb[3][:, :], AF.Sqrt, [pbT[3]], [wkT[0]], scale=1.0 / 256, bias=EPS)
                S.op("dve", lambda e: e.reciprocal(out=wk[0][:, :], in_=wk[0][:, :]), [wkT[0]], [wkT[0]])
                ACT(wk[1][:, :], pb[4][:, :], AF.Sqrt, [pbT[4]], [wkT[1]], scale=1.0 / 128, bias=EPS)
                S.op("dve", lambda e: e.reciprocal(out=wk[1][:, :], in_=wk[1][:, :]), [wkT[1]], [wkT[1]])
                for cc in range(2):
                    S.op("dve", lambda e, cc=cc, cs=cs: e.scalar_tensor_tensor(out=cqT[:, cc, cs], in0=pb[cc][:, :], scalar=qng[:, cc:cc + 1], in1=wk[0][:, :],
                                                                   op0=ALU.mult, op1=ALU.mult), [pbT[cc], wkT[0], tk("qng")], [tk("cqT%d" % g)])
                S.op("dve", lambda e, cs=cs: e.scalar_tensor_tensor(out=ckvT[:, cs], in0=pb[2][:, :], scalar=kvng[:, 0:1], in1=wk[1][:, :],
                                                        op0=ALU.mult, op1=ALU.mult), [pbT[2], wkT[1], tk("kvng")], [tk("ckvT%d" % g)])
                for c in range(8):
                    MM(pb[5][64:96, :], wm[:, c, 384:416], hT[:, c, cs], c == 0, c == 7, [whT[0], hTt[g]], [pbT[5]])
                for c in range(8):
                    MM(pb[6][64:96, :], wm[:, c, 416:448], hT[:, c, cs], c == 0, c == 7, [whT[0], hTt[g]], [pbT[6]])
                TT("dve", wk[2][64:96, :], pb[6][64:96, :], ropeS[64:96, cs], ALU.mult, [pbT[6], tk("rope")], [wkT[2]])
                TT("dve", wk[3][64:96, :], pb[5][64:96, :], ropeC[64:96, cs], ALU.mult, [pbT[5], tk("rope")], [wkT[3]])
                TT("pool", kaug[64:96, cs], wk[2][64:96, :], wk[3][64:96, :], ALU.add, [wkT[2], wkT[3]], [kt_[g]])
            for h in range(8):
                if h + 1 < 8:
                    load_head(h + 1)
                wq_ = wqu_sb[h % 2]
                wv_ = wkv_sb[h % 2]
                for g in range(NG):
                    cs = slice(g * 512, (g + 1) * 512)
                    for cc in range(2):
                        MM(pb[0][0:96, :], wq_[:, cc, 0, :], cqT[:, cc, cs], cc == 0, cc == 1, [wquT[h % 2], tk("cqT%d" % g)], [pbT[0]])
                    for cc in range(2):
                        MM(pb[1][0:96, :], wq_[:, cc, 1, :], cqT[:, cc, cs], cc == 0, cc == 1, [wquT[h % 2], tk("cqT%d" % g)], [pbT[1]])
                    MM(pb[2][0:64, :], wv_[:, 0:64], ckvT[:, cs], True, True, [wkvT[h % 2], tk("ckvT%d" % g)], [pbT[2]])
                    for tt in range(4):
                        MM(pb[3][:, tt * 64:(tt + 1) * 64], ckvT[:, g * 512 + tt * 128:g * 512 + (tt + 1) * 128], wv_[:, 64:128], True, True,
                           [wkvT[h % 2], tk("ckvT%d" % g)], [pbT[3]])
                    rope_combine(96, cs, 0, 1, qaug[0:96, cs], qt_[g])
                    CP("act", kaug[0:64, cs], pb[2][0:64, :], [pbT[2]], [kt_[g]])
                    CP("act", vaug[:, g * 4:(g + 1) * 4, 0:64], pb[3][:, 0:256].rearrange("p (t f) -> p t f", t=4), [pbT[3]], [vt_[g]])
                attention(h, 96, 1.0 / math.sqrt(96.0))

        mla()
        if dbg:
            S.barrier()
            for i_, ap_ in enumerate([qaug, kaug, ropeC, ropeS, ckvT, cqT[:, 0, :], cqT[:, 1, :]]):
                DMA("sp", misc_d[:, i_, :], ap_, [], [tk("dbg_misc")])
            DMA("sp", misc_d[:, 7, 0:2080], vaug.rearrange("p t f -> p (t f)"), [], [tk("dbg_misc")])
            DMA("sp", yb_d[:, :, :], YT4[:, :, :], Yt, [tk("dbg_yb")])
        branch(1)

        S.barrier()
        if "peer" in skip:
            NPG_run = 0
        else:
            NPG_run = NPG
        DMA("pool", keys_sb[:, :, :], keys_d[:, :, :], (), [tk("keys")])
        DMA("sp", fing, fing_d.to_broadcast([128, D]), (), [tk("fing")])
        ubT = [tk("ub%d" % i) for i in range(NUB)]
        vbT = [tk("vb%d" % i) for i in range(NUB)]
        AtT = [tk("At%d" % i) for i in range(NAB)]
        BtT = [tk("Bt%d" % i) for i in range(NAB)]
        geT = [tk("ge0"), tk("ge1")]
        wa2T = [tk("wa20"), tk("wa21")]
        nload = [0]

        def load_uv(i):
            k = nload[0] % NUB
            nload[0] += 1
            DMA("sp", ub[k][:, :, :], us_d[i].rearrange("p (c j) -> p c j", c=8), [stashT[0]], [ubT[k]])
            DMA("sp", vb[k][:, :], vs_d[i], [stashT[1]], [vbT[k]])
            return k

        def bg_gen(G):
            gcs = slice(G * PG, (G + 1) * PG)
            g8 = G // 2
            wpT = [tk("wpq0"), tk("wpq1")]
            DMA("sp", wpq_sb[:, :, 0:128], wpqs_d[:, :, 0:128], [tk("wpqs")], [wpT[0]])
            for pq in range(16):
                hb = pq % 2
                if pq + 1 < 16:
                    DMA("sp", wpq_sb[:, :, (1 - hb) * 128:(2 - hb) * 128], wpqs_d[:, :, (pq + 1) * 128:(pq + 2) * 128], [tk("wpqs")], [wpT[1 - hb]])
                for c in range(8):
                    MM(pb[7][:, hb * PG:(hb + 1) * PG], wpq_sb[:, c, hb * 128:(hb + 1) * 128], hT[:, c, gcs], c == 0, c == 7,
                       [wpT[hb], hTt[g8]], [pbT[7]])
                CP("act", qTs[:, pq, :], pb[7][:, hb * PG:(hb + 1) * PG], [pbT[7]], [tk("qTs")])
                yield
            for tt in range(PG // 128):
                GIJ_ = GIJ if tt == 0 else GIJb
                gijT = tk("GIJ%d" % tt)
                if tt > 0:
                    for _ in range(12):
                        yield
                for b4 in range(4):
                    for q4 in range(4):
                        pq = b4 * 4 + q4
                        MM(pb[7][:, q4 * 128:(q4 + 1) * 128], qTs[:, pq, tt * 128:(tt + 1) * 128], keys_sb[:, pq, :], True, True,
                           [tk("qTs"), tk("keys")], [pbT[7]])
                    CP("act", s_all[:, b4 * 4:(b4 + 1) * 4, :], pb[7][:, :].rearrange("p (q n) -> p q n", q=4), [pbT[7]], [tk("s_all")])
                    yield
                for h in range(8):
                    for half in range(2):
                        src = s_all[:, 2 * h + half, :]
                        vv = v12[:, h, half, :]
                        iu = i12u[:, h, half, :]
                        S.op("dve", lambda e, src=src, vv=vv: e.max(out=vv[:, 0:8], in_=src), [tk("s_all")], [tk("v12")])
                        S.op("dve", lambda e, src=src, vv=vv, iu=iu: e.max_index(out=iu[:, 0:8], in_max=vv[:, 0:8], in_values=src), [tk("s_all"), tk("v12")], [tk("i12u")])
                        S.op("dve", lambda e, src=src, vv=vv: e.match_replace(out=swk[:, 0:128], in_to_replace=vv[:, 0:8], in_values=src, imm_value=-1e30),
                             [tk("s_all"), tk("v12")], [tk("swk")])
                        S.op("dve", lambda e, vv=vv: e.max(out=vv[:, 8:16], in_=swk[:, 0:128]), [tk("swk")], [tk("v12")])
                        S.op("dve", lambda e, vv=vv, iu=iu: e.max_index(out=iu[:, 8:16], in_max=vv[:, 8:16], in_values=swk[:, 0:128]), [tk("swk"), tk("v12")], [tk("i12u")])
                        yield
                CP("dve", i12f[:].rearrange("p a b c -> p (a b c)"), i12u[:].rearrange("p a b c -> p (a b c)"), [tk("i12u")], [tk("i12f")])
                for h in range(8):
                    S.op("dve", lambda e, h=h: e.tensor_tensor(out=cand1.rearrange("p (a b) -> p a b", a=16),
                                                           in0=v12[:, h, 0, :].unsqueeze(2).to_broadcast([128, 16, 16]),
                                                           in1=v12[:, h, 1, :].unsqueeze(1).to_broadcast([128, 16, 16]), op=ALU.add),
                         [tk("v12")], [tk("cand")])
                    src = cand1
                    bv_ = best[:, h, :]
                    pu = posu[:, h, :]
                    S.op("dve", lambda e, src=src, bv_=bv_: e.max(out=bv_[:, 0:8], in_=src), [tk("cand")], [tk("best")])
                    S.op("dve", lambda e, src=src, bv_=bv_, pu=pu: e.max_index(out=pu[:, 0:8], in_max=bv_[:, 0:8], in_values=src), [tk("cand"), tk("best")], [tk("posu")])
                    S.op("dve", lambda e, src=src, bv_=bv_: e.match_replace(out=swk[:, 0:256], in_to_replace=bv_[:, 0:8], in_values=src, imm_value=-1e30),
                         [tk("cand"), tk("best")], [tk("swk")])
                    S.op("dve", lambda e, bv_=bv_: e.max(out=bv_[:, 8:16], in_=swk[:, 0:256]), [tk("swk")], [tk("best")])
                    S.op("dve", lambda e, bv_=bv_, pu=pu: e.max_index(out=pu[:, 8:16], in_max=bv_[:, 8:16], in_values=swk[:, 0:256]), [tk("swk"), tk("best")], [tk("posu")])
                    yield
                pf = posu[:].rearrange("p a b -> p (a b)")
                S.op("dve", lambda e, pf=pf: e.tensor_single_scalar(out=abu[:, 0, :, :].rearrange("p a b -> p (a b)"), in_=pf, scalar=4, op=ALU.logical_shift_right), [tk("posu")], [tk("abu")])
                S.op("dve", lambda e, pf=pf: e.tensor_single_scalar(out=abu[:, 1, :, :].rearrange("p a b -> p (a b)"), in_=pf, scalar=15, op=ALU.bitwise_and), [tk("posu")], [tk("abu")])
                CP("dve", abf[:].rearrange("p a b c -> p (a b c)"), abu[:].rearrange("p a b c -> p (a b c)"), [tk("abu")], [tk("abf")])
                yield
                TT("dve", ez[:], best[:], best[:, :, 0:1].to_broadcast([128, 8, 16]), ALU.subtract, [tk("best")], [tk("ez")])
                for half in range(2):
                    for h in range(8):
                        S.op("dve", lambda e, h=h, half=half: e.tensor_tensor(out=eq[:, h, :, :], in0=abf[:, half, h, :].unsqueeze(2).to_broadcast([128, 16, 16]),
                                                                          in1=iof[:, 0:16].unsqueeze(1).to_broadcast([128, 16, 16]), op=ALU.is_equal),
                             [tk("abf"), tk("iof")], [tk("s_all")])
                        S.op("dve", lambda e, h=h, half=half: e.tensor_tensor(out=eq[:, h, :, :], in0=eq[:, h, :, :],
                                                                          in1=i12f[:, h, half, :].unsqueeze(1).to_broadcast([128, 16, 16]), op=ALU.mult),
                             [tk("s_all"), tk("i12f")], [tk("s_all")])
                        if h % 2 == 1:
                            yield
                    S.op("dve", lambda e, half=half, GIJ_=GIJ_: e.tensor_reduce(out=GIJ_[:, 1 + half, :], in_=eq[:].rearrange("p h k a -> p (h k) a"), axis=AX.X, op=ALU.add),
                         [tk("s_all")], [gijT])
                ACT(ez[:].rearrange("p a b -> p (a b)"), ez[:].rearrange("p a b -> p (a b)"), AF.Exp, [tk("ez")], [tk("ez")])
                S.op("dve", lambda e: e.tensor_reduce(out=zs[:, 0:8], in_=ez[:], axis=AX.X, op=ALU.add), [tk("ez")], [tk("zs")])
                S.op("dve", lambda e: e.reciprocal(out=zs[:, 8:16], in_=zs[:, 0:8]), [tk("zs")], [tk("zs")])
                TT("dve", GIJ_[:, 0, :].rearrange("p (h k) -> p h k", h=8), ez[:], zs[:, 8:16].unsqueeze(2).to_broadcast([128, 8, 16]), ALU.mult,
                   [tk("ez"), tk("zs")], [gijT])
                yield
            for _ in range(6):
                yield
            for tt in range(PG // 128):
                GIJ_ = GIJ if tt == 0 else GIJb
                gijT = tk("GIJ%d" % tt)
                for q in range(3):
                    TR(pb[7][:, q * 128:(q + 1) * 128], GIJ_[:, q, :], identf[:], [gijT, tk("identf")], [pbT[7]])
                CP("act", GIJT2[tt][:].rearrange("p a b -> p (a b)"), pb[7][:, 0:384], [pbT[7]], [tk("GIJT%d" % tt)])
                yield

        def scatter(G):
            for tt in range(PG // 128):
                gt = GIJT2[tt]
                gtT = tk("GIJT%d" % tt)
                for t4 in range(32):
                    b = next_bank(2)
                    for u in range(4):
                        t = t4 * 4 + u
                        ka = (t) % NAB
                        TS("dve", At[ka][:, :], iob[:, :], gt[:, 1, t:t + 1], gt[:, 0, t:t + 1], ALU.is_equal, ALU.mult,
                           [tk("iob"), gtT], [AtT[ka]])
                        TS("dve", Bt[ka][:, :], iob[:, :], gt[:, 2, t:t + 1], None, ALU.is_equal, None, [tk("iob"), gtT], [BtT[ka]])
                        MM(pb[b][:, u * 128:(u + 1) * 128], Bt[ka][:, :], At[ka][:, :], True, True, [AtT[ka], BtT[ka]], [pbT[b]])
                    tg = tt * 128 + t4 * 4
                    CP("act", Wbuf[:, tg:tg + 4, :], pb[b][:, :].rearrange("p (t i) -> p t i", t=4), [pbT[b]], [tk("Wbuf")])

        def dense(G, bg):
            gcs = slice(G * PG, (G + 1) * PG)
            g8 = G // 2
            pend = [load_uv(0), load_uv(1), load_uv(2)]
            abank = [0, 1, 6]

            def emit_u(i):
                k = pend[i]
                b = abank[i % 3]
                for c in range(8):
                    MM(pb[b][:, 0:PG], ub[k][:, c, :], hT[:, c, gcs], c == 0, c == 7, [ubT[k], hTt[g8]], [pbT[b]])

            emit_u(0)
            emit_u(1)
            for i in range(128):
                if i + 3 < 128:
                    pend.append(load_uv(i + 3))
                k = pend[i]
                b = abank[i % 3]
                b2 = i % 2
                if i + 2 < 128:
                    emit_u(i + 2)
                ACT(geb[b2][:, :], pb[b][:, 0:PG], AF.Gelu, [pbT[b]], [geT[b2]])
                TT("pool", wab2[b2][:, :], geb[b2][:, :], Wbuf[:, :, i], ALU.mult, [geT[b2], tk("Wbuf")], [wa2T[b2]])
                for tt in range(PG // 128):
                    for dh in range(2):
                        bo = 2 + tt * 2 + dh
                        MM(pb[bo][:, :], wab2[b2][:, tt * 128:(tt + 1) * 128], vb[k][:, dh * 512:(dh + 1) * 512], i == 0, i == 127,
                           [wa2T[b2], vbT[k]], [pbT[bo]])
                if bg is not None and i >= 2:
                    next(bg, None)
            if bg is not None:
                for _ in bg:
                    pass

        def final(G):
            for tt in range(PG // 128):
                ti = G * (PG // 128) + tt
                xb = xt[ti % 2]
                xbT = xtT[ti % 2]
                DMA("sp", xb[:], x1_d[ti * 128:(ti + 1) * 128, :], [tk("x1s%d" % ti)], [xbT])
                if dbg:
                    for dh in range(2):
                        CP("dve", xwP[:, dh * 512:(dh + 1) * 512], pb[2 + tt * 2 + dh][:, :], [pbT[2 + tt * 2 + dh]], [tk("s_all")])
                    DMA("sp", pe_d[ti * 128:(ti + 1) * 128, :], xwP[:], [tk("s_all")], [tk("dbg_pe")])
                for dh in range(2):
                    bo = 2 + tt * 2 + dh
                    TT("dve", xb[:, dh * 512:(dh + 1) * 512], pb[bo][:, :], xb[:, dh * 512:(dh + 1) * 512], ALU.add, [pbT[bo], xbT], [xbT])
                ACT(xwP[:], xb[:], AF.Square, [xbT], [tk("s_all")])
                S.op("dve", lambda e: e.tensor_reduce(out=sm[:, 8:9], in_=xwP[:], axis=AX.X, op=ALU.add), [tk("s_all")], [tk("sm8")])
                ACT(sm[:, 9:10], sm[:, 8:9], AF.Sqrt, [tk("sm8")], [tk("sm9")], scale=1.0 / D, bias=EPS)
                S.op("dve", lambda e: e.reciprocal(out=sm[:, 10:11], in_=sm[:, 9:10]), [tk("sm9")], [tk("sm10")])
                S.op("dve", lambda e, xb=xb: e.scalar_tensor_tensor(out=xb[:], in0=xb[:], scalar=sm[:, 10:11], in1=fing, op0=ALU.mult, op1=ALU.mult),
                     [xbT, tk("sm10"), tk("fing")], [xbT])
                DMA("sp", out_d[ti * 128:(ti + 1) * 128, :], xb[:], [xbT], [tk("out")])

        if NPG_run:
            for _ in bg_gen(0):
                pass
        for G in range(NPG_run):
            scatter(G)
            dense(G, bg_gen(G + 1) if G + 1 < NPG_run else None)
            final(G)

        S.barrier()
        S.emit()
    return nc


def _prep_inputs(inputs):
    f = np.float32
    w_in = np.asarray(inputs["w_in"][0], f)
    cols = []
    qa, ka, va = w_in[:, 0:512], w_in[:, 512:1024], w_in[:, 1024:1536]
    for h in range(8):
        q = qa[:, h * 64:(h + 1) * 64]
        k = ka[:, h * 64:(h + 1) * 64]
        cols += [q, np.concatenate([q[:, 32:], q[:, :32]], 1), k, np.concatenate([k[:, 32:], k[:, :32]], 1), va[:, h * 64:(h + 1) * 64]]
    cols.append(w_in[:, 1536:1792])
    cols.append(w_in[:, 1792:1920])
    kpe = w_in[:, 1920:1952]
    cols += [kpe, np.concatenate([kpe[:, 16:], kpe[:, :16]], 1)]
    cols.append(w_in[:, 1952:4000])
    w_inr = np.concatenate(cols, 1)
    assert w_inr.shape[1] == NCOL
    w_inr = np.ascontiguousarray(w_inr.reshape(8, 128, NCOL).transpose(1, 0, 2))

    def fm(v):
        return np.ascontiguousarray(np.asarray(v, f).reshape(-1, 128).T)

    wqu = np.asarray(inputs["w_q_up"][0], f).reshape(2, 128, 8, 96)
    nrm = wqu
    swp = np.concatenate([wqu[..., :64], wqu[..., 80:96], wqu[..., 64:80]], -1)
    wqu_r = np.stack([nrm, swp], 0)
    wqu_r = np.ascontiguousarray(wqu_r.transpose(2, 3, 1, 0, 4))
    wkv = np.ascontiguousarray(np.asarray(inputs["w_kv_up"][0], f).reshape(128, 8, 128))

    def rowchunks(w):
        w = np.asarray(w, f)
        return np.ascontiguousarray(w.reshape(-1, 128, w.shape[1]).transpose(1, 0, 2))

    k1 = np.asarray(inputs["peer_sub_keys_1"][0], f)
    k2 = np.asarray(inputs["peer_sub_keys_2"][0], f)
    keys = np.stack([k1, k2], 1).reshape(16, 128, 128)
    keysT = np.ascontiguousarray(keys.transpose(2, 0, 1))
    u = np.asarray(inputs["peer_expert_u"][0], f)
    uT = np.ascontiguousarray(u.reshape(128, 128, 8, 128).transpose(0, 3, 2, 1)).reshape(128, 128, 1024)
    vE = np.ascontiguousarray(np.asarray(inputs["peer_expert_v"][0], f).reshape(128, 128, 1024))
    shared = {
        "w_inr": w_inr,
        "mixg": fm(inputs["mix_norm_g"][0]),
        "ffng": fm(inputs["ffn_norm_g"][0]),
        "fing": np.ascontiguousarray(np.asarray(inputs["final_norm_g"], f).reshape(1, D)),
        "qng": fm(inputs["q_norm_g"][0]),
        "kvng": fm(inputs["kv_norm_g"][0]),
        "wqu": wqu_r,
        "wkv": wkv,
        "wa": rowchunks(inputs["w_branch_a"][0]),
        "wb": rowchunks(inputs["w_branch_b"][0]),
        "wo": rowchunks(inputs["w_out"][0]),
        "wpq": rowchunks(inputs["w_peer_query"][0]),
        "keysT": keysT,
        "uT": uT,
        "vE": vE,
    }
    return shared


def kernel(**inputs):
    n = 8
    shared = _prep_inputs(inputs)
    x = np.asarray(inputs["x"], np.float32)
    pos = np.asarray(inputs["positions"], np.int32)
    in_maps = []
    for c in range(n):
        m = dict(shared)
        m["x"] = np.ascontiguousarray(x[c])
        m["pos"] = np.ascontiguousarray(pos[c].reshape(1, S_LEN))
        in_maps.append(m)
    nc = build_nc()
    res = run_bass_kernel_spmd(nc, in_maps, core_ids=list(range(n)))
    return np.stack([np.asarray(r["out"], np.float32) for r in res.results], 0)
```
